# Optimizing a Trainium2 kernel written in Bass

```python
import jax, jax.numpy as jnp
from jax import lax
import numpy as np

D_MODEL = 1024
BATCH = 8
SEQ = 2048
DEPTH = 2
DEC_BATCH = 128
DEC_SEQ = 1
PAST_LEN = 8192
PAGE_SIZE = 128

EPS = 1e-6
D_A = D_MODEL
CONV_A_W = 3
D_SSM = 2 * D_MODEL
SSM_HEADDIM = 64
H_SSM = D_SSM // SSM_HEADDIM
G_SSM = 8
N_SSM = 128
SSM_CONV_W = 4
SSM_CONV_DIM = D_SSM + 2 * G_SSM * N_SSM
SSD_CHUNK = 128
H_ATT = 16
KV_ATT = 4
HD_ATT = 64
ROT_DIM = HD_ATT // 4
ROPE_THETA = 500000.0
WINDOW = 128
MEM_LEN = 256
X_H = 4
X_HD = D_MODEL // X_H
D_FF = 2816
FFN_CONV_W = 3
N_BRANCH = 3
IN_SIZES = (D_A, D_A, D_A, D_SSM, SSM_CONV_DIM, H_SSM, H_ATT * HD_ATT, KV_ATT * HD_ATT, KV_ATT * HD_ATT, N_BRANCH * D_MODEL)
IN_DIM = sum(IN_SIZES)

kernel_name = 'hybrid_gated_parallel_decoder_step'


def split_cols(z, sizes):
    idx = np.cumsum(np.array(sizes))[:-1].tolist()
    return jnp.split(z, idx, axis=-1)


def rmsnorm(x, g):
    xf = x.astype(jnp.float32)
    y = xf * lax.rsqrt(jnp.mean(xf * xf, axis=-1, keepdims=True) + EPS)
    return (y * g.astype(jnp.float32)).astype(x.dtype)


def causal_dwconv(x, buf, w, b=None):
    L = x.shape[1]
    xx = jnp.concatenate([buf.astype(x.dtype), x], axis=1)
    y = xx[:, 0:L] * w[0]
    for k in range(1, w.shape[0]):
        y = y + xx[:, k:k + L] * w[k]
    if b is not None:
        y = y + b
    return y, xx[:, L:]


def partial_rope(x, pos):
    half = ROT_DIM // 2
    inv = ROPE_THETA ** (-jnp.arange(half, dtype=jnp.float32) / half)
    ang = pos.astype(jnp.float32)[:, None] * inv[None, :]
    cos = jnp.cos(ang)[None, :, None, :]
    sin = jnp.sin(ang)[None, :, None, :]
    xr = x[..., :ROT_DIM].astype(jnp.float32)
    x1, x2 = xr[..., :half], xr[..., half:]
    rot = jnp.concatenate([x1 * cos - x2 * sin, x2 * cos + x1 * sin], axis=-1)
    return jnp.concatenate([rot.astype(x.dtype), x[..., ROT_DIM:]], axis=-1)


def sink_attention(q, k, v, q_pos, k_pos, sinks):
    s = jnp.einsum('bnqgrd,bnkgd->bngrqk', q, k).astype(jnp.float32) * (HD_ATT ** -0.5)
    diff = q_pos[:, :, None] - k_pos[:, None, :]
    mask = (diff >= 0) & (diff < WINDOW) & (k_pos[:, None, :] >= 0)
    s = jnp.where(mask[None, :, None, None], s, -jnp.inf)
    sink = sinks.astype(jnp.float32).reshape(KV_ATT, H_ATT // KV_ATT)[None, None, :, :, None, None]
    m = jnp.maximum(jnp.max(s, axis=-1, keepdims=True), sink)
    e = jnp.exp(s - m)
    prob = e / (jnp.sum(e, axis=-1, keepdims=True) + jnp.exp(sink - m))
    return jnp.einsum('bngrqk,bnkgd->bnqgrd', prob.astype(v.dtype), v)


def ssd_scan(x, dt, A, B, C, h0):
    b, L, H, P = x.shape
    G, N = B.shape[2], B.shape[3]
    R = H // G
    Q = min(SSD_CHUNK, L)
    nc = -(-L // Q)
    pad = nc * Q - L
    f32 = jnp.float32
    x = jnp.pad(x.astype(f32), ((0, 0), (0, pad), (0, 0), (0, 0))).reshape(b, nc, Q, G, R, P)
    dt = jnp.pad(dt.astype(f32), ((0, 0), (0, pad), (0, 0))).reshape(b, nc, Q, G, R)
    B = jnp.pad(B.astype(f32), ((0, 0), (0, pad), (0, 0), (0, 0))).reshape(b, nc, Q, G, N)
    C = jnp.pad(C.astype(f32), ((0, 0), (0, pad), (0, 0), (0, 0))).reshape(b, nc, Q, G, N)
    cs = jnp.cumsum(dt * A.reshape(G, R), axis=2)
    xdt = x * dt[..., None]
    seg = cs[:, :, :, None] - cs[:, :, None, :]
    causal = jnp.tril(jnp.ones((Q, Q), dtype=bool))[:, :, None, None]
    Lmat = jnp.exp(jnp.where(causal, seg, -jnp.inf))
    CB = jnp.einsum('bclgn,bcsgn->bclsg', C, B)
    y_diag = jnp.einsum('bclsg,bclsgr,bcsgrp->bclgrp', CB, Lmat, xdt)
    decay_s = jnp.exp(cs[:, :, -1:] - cs)
    states = jnp.einsum('bcsgn,bcsgr,bcsgrp->bcgrpn', B, decay_s, xdt)
    chunk_decay = jnp.exp(cs[:, :, -1])

    def step(hc, inp):
        st, dec = inp
        return hc * dec[..., None, None] + st, hc

    h_last, h_prev = lax.scan(step, h0.astype(f32).reshape(b, G, R, P, N),
                              (jnp.moveaxis(states, 1, 0), jnp.moveaxis(chunk_decay, 1, 0)))
    h_prev = jnp.moveaxis(h_prev, 0, 1)
    y_off = jnp.einsum('bclgn,bcgrpn,bclgr->bclgrp', C, h_prev, jnp.exp(cs))
    y = (y_diag + y_off).reshape(b, nc * Q, H, P)[:, :L]
    return y, h_last.reshape(b, H, P, N)


def gated_parallel_mixer(h, pos, buf_a, buf_ssm, ssm0, swa_k, swa_v, p):
    b, L, _ = h.shape
    (v_a, gb_a, gc_a, z_s, xbc, dt_raw, q, k, v, gates) = split_cols(h @ p['w_in'], IN_SIZES)
    u_a, new_buf_a = causal_dwconv(gc_a * v_a, buf_a, p['conv_a_w'])
    y_a = (gb_a * u_a) @ p['w_a_out']
    xbc, new_buf_ssm = causal_dwconv(xbc, buf_ssm, p['ssm_conv_w'], p['ssm_conv_b'])
    xbc = jax.nn.silu(xbc)
    xs, Bs, Cs = split_cols(xbc, (D_SSM, G_SSM * N_SSM, G_SSM * N_SSM))
    dt = jax.nn.softplus((dt_raw + p['ssm_dt_bias']).astype(jnp.float32))
    A = -jnp.exp(p['ssm_a_log'].astype(jnp.float32))
    xh = xs.reshape(b, L, H_SSM, SSM_HEADDIM)
    y_s, new_ssm = ssd_scan(xh, dt, A, Bs.reshape(b, L, G_SSM, N_SSM), Cs.reshape(b, L, G_SSM, N_SSM), ssm0)
    y_s = (y_s + xh.astype(jnp.float32) * p['ssm_d'].astype(jnp.float32)[:, None]).astype(h.dtype)
    y_s = rmsnorm(y_s.reshape(b, L, D_SSM) * jax.nn.silu(z_s), p['ssm_norm'])
    y_b = y_s @ p['w_ssm_out']
    q = partial_rope(q.reshape(b, L, H_ATT, HD_ATT), pos).reshape(b, L, KV_ATT, H_ATT // KV_ATT, HD_ATT)
    k = partial_rope(k.reshape(b, L, KV_ATT, HD_ATT), pos)
    v = v.reshape(b, L, KV_ATT, HD_ATT)
    if swa_k is None:
        nb = L // WINDOW
        qb = q.reshape(b, nb, WINDOW, KV_ATT, H_ATT // KV_ATT, HD_ATT)
        zpad = jnp.zeros((b, WINDOW, KV_ATT, HD_ATT), k.dtype)
        kp = jnp.concatenate([zpad, k], axis=1).reshape(b, nb + 1, WINDOW, KV_ATT, HD_ATT)
        vp = jnp.concatenate([zpad, v], axis=1).reshape(b, nb + 1, WINDOW, KV_ATT, HD_ATT)
        kb = jnp.concatenate([kp[:, :-1], kp[:, 1:]], axis=2)
        vb = jnp.concatenate([vp[:, :-1], vp[:, 1:]], axis=2)
        q_pos = pos.reshape(nb, WINDOW)
        k_pos = (jnp.arange(nb)[:, None] - 1) * WINDOW + jnp.arange(2 * WINDOW)[None, :]
        o = sink_attention(qb, kb, vb, q_pos, k_pos, p['attn_sinks'])
        new_k, new_v = k[:, L - WINDOW:], v[:, L - WINDOW:]
    else:
        n_buf = swa_k.shape[1]
        kk = jnp.concatenate([swa_k.astype(k.dtype), k], axis=1)
        vv = jnp.concatenate([swa_v.astype(v.dtype), v], axis=1)
        k_pos = pos[0] - n_buf + jnp.arange(n_buf + L)
        o = sink_attention(q[:, None], kk[:, None], vv[:, None], pos[None], k_pos[None], p['attn_sinks'])
        new_k, new_v = kk[:, L:], vv[:, L:]
    y_c = o.reshape(b, L, H_ATT * HD_ATT) @ p['w_attn_out']
    g_a, g_b, g_c = split_cols(jax.nn.sigmoid(gates), (D_MODEL, D_MODEL, D_MODEL))
    out = (g_a * y_a + g_b * y_b + g_c * y_c) @ p['w_out']
    return out, (new_buf_a, new_buf_ssm, new_ssm, new_k, new_v)


def memory_kv(mem, p):
    b, M, _ = mem.shape
    m = rmsnorm(mem, p['norm_mem'])
    return (m @ p['w_xk']).reshape(b, M, X_H, X_HD), (m @ p['w_xv']).reshape(b, M, X_H, X_HD)


def cross_attention(h, mem_k, mem_v, p):
    b, L, _ = h.shape
    q = (h @ p['w_xq']).reshape(b, L, X_H, X_HD)
    s = jnp.einsum('bqhd,bkhd->bhqk', q, mem_k.astype(h.dtype)).astype(jnp.float32) * (X_HD ** -0.5)
    a = jax.nn.softmax(s, axis=-1).astype(h.dtype)
    o = jnp.einsum('bhqk,bkhd->bqhd', a, mem_v.astype(h.dtype)).reshape(b, L, X_H * X_HD)
    return o @ p['w_xo']


def conv_ffn(h, buf, p):
    a, g = split_cols(h @ p['w_ffn_in'], (D_FF, D_FF))
    a, new_buf = causal_dwconv(a, buf, p['ffn_conv_w'], p['ffn_conv_b'])
    return (jax.nn.silu(a) * g) @ p['w_ffn_out'], new_buf


def decoder_layer(x, pos, st, mem_k, mem_v, p):
    buf_a, buf_ssm, ssm0, swa_k, swa_v, buf_ffn = st
    m, mix_state = gated_parallel_mixer(rmsnorm(x, p['norm_mix_pre']), pos, buf_a, buf_ssm, ssm0, swa_k, swa_v, p)
    x = x + rmsnorm(m, p['norm_mix_post'])
    x = x + rmsnorm(cross_attention(rmsnorm(x, p['norm_x_pre']), mem_k, mem_v, p), p['norm_x_post'])
    f, new_buf_ffn = conv_ffn(rmsnorm(x, p['norm_ffn_pre']), buf_ffn, p)
    x = x + rmsnorm(f, p['norm_ffn_post'])
    return x, mix_state + (new_buf_ffn,)


def setup_inputs(seed: int = 0) -> dict:
    key = jax.random.key(seed)
    ks = list(jax.random.split(key, 48))

    def nrm(i, shape, scale):
        return scale * jax.random.normal(ks[i], shape, jnp.float32)

    def gain(i, shape):
        return 1.0 + nrm(i, shape, 0.05)

    n_rows = min(WINDOW, PAST_LEN)
    dt0 = jnp.exp(jax.random.uniform(ks[40], (DEPTH, H_SSM), jnp.float32, np.log(1e-3), np.log(1e-1)))
    return {
        'x_prompt': nrm(0, (BATCH, SEQ, D_MODEL), 1.0),
        'x_sample': nrm(1, (DEC_BATCH, DEC_SEQ, D_MODEL), 1.0),
        'mem_prompt': nrm(2, (BATCH, MEM_LEN, D_MODEL), 1.0),
        'state_conv_a': nrm(3, (DEPTH, DEC_BATCH, CONV_A_W - 1, D_A), 1.0),
        'state_ssm_conv': nrm(4, (DEPTH, DEC_BATCH, SSM_CONV_W - 1, SSM_CONV_DIM), 1.0),
        'state_ssm': nrm(5, (DEPTH, DEC_BATCH, H_SSM, SSM_HEADDIM, N_SSM), 0.1),
        'cache_swa_k': nrm(6, (DEPTH, DEC_BATCH, n_rows, KV_ATT, HD_ATT), 1.0),
        'cache_swa_v': nrm(7, (DEPTH, DEC_BATCH, n_rows, KV_ATT, HD_ATT), 1.0),
        'cache_mem_k': nrm(8, (DEPTH, DEC_BATCH, MEM_LEN, X_H, X_HD), 1.0),
        'cache_mem_v': nrm(9, (DEPTH, DEC_BATCH, MEM_LEN, X_H, X_HD), 1.0),
        'state_ffn_conv': nrm(10, (DEPTH, DEC_BATCH, FFN_CONV_W - 1, D_FF), 1.0),
        'norm_mix_pre': gain(11, (DEPTH, D_MODEL)),
        'norm_mix_post': gain(12, (DEPTH, D_MODEL)),
        'w_in': nrm(13, (DEPTH, D_MODEL, IN_DIM), D_MODEL ** -0.5),
        'conv_a_w': nrm(14, (DEPTH, CONV_A_W, D_A), CONV_A_W ** -0.5),
        'w_a_out': nrm(15, (DEPTH, D_A, D_MODEL), D_A ** -0.5),
        'ssm_conv_w': nrm(16, (DEPTH, SSM_CONV_W, SSM_CONV_DIM), SSM_CONV_W ** -0.5),
        'ssm_conv_b': nrm(17, (DEPTH, SSM_CONV_DIM), 0.02),
        'ssm_dt_bias': dt0 + jnp.log(-jnp.expm1(-dt0)),
        'ssm_a_log': jnp.log(jax.random.uniform(ks[18], (DEPTH, H_SSM), jnp.float32, 1.0, 16.0)),
        'ssm_d': 1.0 + nrm(19, (DEPTH, H_SSM), 0.1),
        'ssm_norm': gain(20, (DEPTH, D_SSM)),
        'w_ssm_out': nrm(21, (DEPTH, D_SSM, D_MODEL), D_SSM ** -0.5),
        'attn_sinks': nrm(22, (DEPTH, H_ATT), 0.5),
        'w_attn_out': nrm(23, (DEPTH, H_ATT * HD_ATT, D_MODEL), (H_ATT * HD_ATT) ** -0.5),
        'w_out': nrm(24, (DEPTH, D_MODEL, D_MODEL), D_MODEL ** -0.5),
        'norm_x_pre': gain(25, (DEPTH, D_MODEL)),
        'norm_x_post': gain(26, (DEPTH, D_MODEL)),
        'norm_mem': gain(27, (DEPTH, D_MODEL)),
        'w_xq': nrm(28, (DEPTH, D_MODEL, X_H * X_HD), D_MODEL ** -0.5),
        'w_xk': nrm(29, (DEPTH, D_MODEL, X_H * X_HD), D_MODEL ** -0.5),
        'w_xv': nrm(30, (DEPTH, D_MODEL, X_H * X_HD), D_MODEL ** -0.5),
        'w_xo': nrm(31, (DEPTH, X_H * X_HD, D_MODEL), (X_H * X_HD) ** -0.5),
        'norm_ffn_pre': gain(32, (DEPTH, D_MODEL)),
        'norm_ffn_post': gain(33, (DEPTH, D_MODEL)),
        'w_ffn_in': nrm(34, (DEPTH, D_MODEL, 2 * D_FF), D_MODEL ** -0.5),
        'ffn_conv_w': nrm(35, (DEPTH, FFN_CONV_W, D_FF), FFN_CONV_W ** -0.5),
        'ffn_conv_b': nrm(36, (DEPTH, D_FF), 0.02),
        'w_ffn_out': nrm(37, (DEPTH, D_FF, D_MODEL), D_FF ** -0.5),
    }


def reference(x_prompt, x_sample, mem_prompt, state_conv_a, state_ssm_conv, state_ssm, cache_swa_k, cache_swa_v,
              cache_mem_k, cache_mem_v, state_ffn_conv, norm_mix_pre, norm_mix_post, w_in, conv_a_w, w_a_out,
              ssm_conv_w, ssm_conv_b, ssm_dt_bias, ssm_a_log, ssm_d, ssm_norm, w_ssm_out, attn_sinks, w_attn_out,
              w_out, norm_x_pre, norm_x_post, norm_mem, w_xq, w_xk, w_xv, w_xo, norm_ffn_pre, norm_ffn_post,
              w_ffn_in, ffn_conv_w, ffn_conv_b, w_ffn_out):
    bp, Lp, _ = x_prompt.shape
    Ls = x_sample.shape[1]
    pos_p = jnp.arange(Lp, dtype=jnp.int32)
    pos_s = PAST_LEN + jnp.arange(Ls, dtype=jnp.int32)
    yp, ys = x_prompt, x_sample
    new_p, new_s = [], []
    for l in range(DEPTH):
        p = {'norm_mix_pre': norm_mix_pre[l], 'norm_mix_post': norm_mix_post[l], 'w_in': w_in[l],
             'conv_a_w': conv_a_w[l], 'w_a_out': w_a_out[l], 'ssm_conv_w': ssm_conv_w[l],
             'ssm_conv_b': ssm_conv_b[l], 'ssm_dt_bias': ssm_dt_bias[l], 'ssm_a_log': ssm_a_log[l],
             'ssm_d': ssm_d[l], 'ssm_norm': ssm_norm[l], 'w_ssm_out': w_ssm_out[l], 'attn_sinks': attn_sinks[l],
             'w_attn_out': w_attn_out[l], 'w_out': w_out[l], 'norm_x_pre': norm_x_pre[l],
             'norm_x_post': norm_x_post[l], 'norm_mem': norm_mem[l], 'w_xq': w_xq[l], 'w_xk': w_xk[l],
             'w_xv': w_xv[l], 'w_xo': w_xo[l], 'norm_ffn_pre': norm_ffn_pre[l], 'norm_ffn_post': norm_ffn_post[l],
             'w_ffn_in': w_ffn_in[l], 'ffn_conv_w': ffn_conv_w[l], 'ffn_conv_b': ffn_conv_b[l],
             'w_ffn_out': w_ffn_out[l]}
        zero_st = (jnp.zeros((bp, CONV_A_W - 1, D_A), yp.dtype),
                   jnp.zeros((bp, SSM_CONV_W - 1, SSM_CONV_DIM), yp.dtype),
                   jnp.zeros((bp, H_SSM, SSM_HEADDIM, N_SSM), jnp.float32),
                   None, None,
                   jnp.zeros((bp, FFN_CONV_W - 1, D_FF), yp.dtype))
        mk, mv = memory_kv(mem_prompt, p)
        yp, st_p = decoder_layer(yp, pos_p, zero_st, mk, mv, p)
        new_p.append(st_p + (mk, mv))
        st_in = (state_conv_a[l], state_ssm_conv[l], state_ssm[l], cache_swa_k[l], cache_swa_v[l], state_ffn_conv[l])
        ys, st_s = decoder_layer(ys, pos_s, st_in, cache_mem_k[l], cache_mem_v[l], p)
        new_s.append(st_s)
    p_conv_a = jnp.stack([s[0] for s in new_p])
    p_ssm_conv = jnp.stack([s[1] for s in new_p])
    p_ssm = jnp.stack([s[2] for s in new_p])
    p_swa_k = jnp.stack([s[3] for s in new_p])
    p_swa_v = jnp.stack([s[4] for s in new_p])
    p_ffn_conv = jnp.stack([s[5] for s in new_p])
    p_mem_k = jnp.stack([s[6] for s in new_p])
    p_mem_v = jnp.stack([s[7] for s in new_p])
    s_conv_a = jnp.stack([s[0] for s in new_s])
    s_ssm_conv = jnp.stack([s[1] for s in new_s])
    s_ssm = jnp.stack([s[2] for s in new_s])
    s_swa_k = jnp.stack([s[3] for s in new_s])
    s_swa_v = jnp.stack([s[4] for s in new_s])
    s_ffn_conv = jnp.stack([s[5] for s in new_s])
    return (yp, ys, p_conv_a, p_ssm_conv, p_ssm, p_swa_k, p_swa_v, p_ffn_conv, p_mem_k, p_mem_v,
            s_conv_a, s_ssm_conv, s_ssm, s_swa_k, s_swa_v, s_ffn_conv)
```

```python
import numpy as np
import concourse.bass as bass
import concourse.mybir as mybir
from concourse.bass_utils import run_bass_kernel_spmd

F32 = mybir.dt.float32
BF16 = mybir.dt.bfloat16
AF = mybir.ActivationFunctionType
ALU = mybir.AluOpType
AX = mybir.AxisListType

GRAN = 128


class V:
    def __init__(self, ap, keys):
        self.ap = ap
        self.keys = tuple(keys)

    def __getitem__(self, idx):
        return V(self.ap[idx], self.keys)


class Op:
    __slots__ = ("eng", "fn", "reads", "writes", "inc", "dma", "dsem")

    def __init__(self, eng, fn, reads, writes, inc=True, dma=False, dsem=None):
        self.eng = eng
        self.fn = fn
        self.reads = reads
        self.writes = writes
        self.inc = inc
        self.dma = dma
        self.dsem = dsem


def _keys(lst):
    out = []
    for x in lst:
        if x is None:
            continue
        if isinstance(x, V):
            out.extend(x.keys)
        elif isinstance(x, (list, tuple)) and len(x) > 0 and isinstance(x[0], V):
            for y in x:
                out.extend(y.keys)
        else:
            out.append(x)
    return out


class Prog:
    ENGS = ("pe", "act", "dve", "pool", "sp")

    def __init__(self, nc, arena_bytes, n_dsem=6):
        self.nc = nc
        self.ops = []
        self.cur = self.ops
        self.arena_bytes = arena_bytes
        self.arena_off = 0
        self.n_dsem = n_dsem
        self.arena = None
        self.psum_banks = []
        self.dram_cnt = 0

    def alloc(self, shape, dtype, name=None):
        esz = 4 if dtype == F32 else 2
        n = int(np.prod(shape))
        nbytes = n * esz
        nb = (nbytes + GRAN - 1) // GRAN * GRAN
        off = self.arena_off
        self.arena_off += nb
        assert self.arena_off <= self.arena_bytes, ("SBUF arena overflow", self.arena_off)
        ap = self.arena[:, off // 2:(off + nbytes) // 2]
        if dtype == F32:
            ap = ap.bitcast(F32)
        if len(shape) == 2:
            ap = ap.rearrange("p (a b) -> p a b", a=shape[0])
        elif len(shape) == 3:
            ap = ap.rearrange("p (a b c) -> p a b c", a=shape[0], b=shape[1])
        keys = [("sb", g) for g in range(off // GRAN, (off + nb) // GRAN)]
        return V(ap, keys)

    def allocn(self, n, shape, dtype):
        return [self.alloc(shape, dtype) for _ in range(n)]

    def op(self, eng, fn, reads=(), writes=(), inc=True):
        self.cur.append(Op(eng, fn, _keys(reads), _keys(writes), inc=inc))

    def dma(self, q, out_ap, in_ap, reads=(), writes=(), dsem=0):
        def fn(e, out_ap=out_ap, in_ap=in_ap):
            return e.dma_start(out=out_ap, in_=in_ap)
        self.cur.append(Op(q, fn, _keys(reads), _keys(writes), dma=True, dsem=dsem))

    def slot(self):
        s = []
        self.cur.append(s)
        return s

    def mm(self, out, lhsT, rhs, start, stop, inc=None, **kw):
        if inc is None:
            inc = stop
        def fn(e, o=out.ap, l=lhsT.ap, r=rhs.ap):
            return e.matmul(o, lhsT=l, rhs=r, start=start, stop=stop, **kw)
        self.op("pe", fn, reads=[lhsT, rhs], writes=[out], inc=inc)

    def transpose(self, out, in_, ident, inc=True):
        def fn(e, o=out.ap, i=in_.ap, d=ident.ap):
            return e.transpose(o, i, d)
        self.op("pe", fn, reads=[in_, ident], writes=[out], inc=inc)

    def act(self, out, in_, func, bias=None, scale=1.0, accum=None, extra_reads=()):
        def fn(e, o=out.ap, i=in_.ap):
            kw = {}
            if bias is not None:
                kw["bias"] = bias.ap if isinstance(bias, V) else bias
            if accum is not None:
                kw["accum_out"] = accum.ap
            sc = scale.ap if isinstance(scale, V) else scale
            return e.activation(out=o, in_=i, func=func, scale=sc, **kw)
        rd = [in_] + [b for b in (bias, scale) if isinstance(b, V)] + list(extra_reads)
        wr = [out] + ([accum] if accum is not None else [])
        self.op("act", fn, reads=rd, writes=wr)

    def tt(self, eng, out, in0, in1, op):
        def fn(e, o=out.ap, a=in0.ap, b=in1.ap):
            return e.tensor_tensor(out=o, in0=a, in1=b, op=op)
        self.op(eng, fn, reads=[in0, in1], writes=[out])

    def ts(self, eng, out, in0, s1, op0, s2=None, op1=None):
        def fn(e, o=out.ap, a=in0.ap):
            a1 = s1.ap if isinstance(s1, V) else s1
            a2 = s2.ap if isinstance(s2, V) else s2
            if op1 is None:
                return e.tensor_scalar(out=o, in0=a, scalar1=a1, scalar2=None, op0=op0)
            return e.tensor_scalar(out=o, in0=a, scalar1=a1, scalar2=a2, op0=op0, op1=op1)
        rd = [in0] + [s for s in (s1, s2) if isinstance(s, V)]
        self.op(eng, fn, reads=rd, writes=[out])

    def stt(self, eng, out, in0, scalar, in1, op0, op1):
        def fn(e, o=out.ap, a=in0.ap, b=in1.ap):
            s = scalar.ap if isinstance(scalar, V) else scalar
            return e.scalar_tensor_tensor(out=o, in0=a, scalar=s, in1=b, op0=op0, op1=op1)
        rd = [in0, in1] + ([scalar] if isinstance(scalar, V) else [])
        self.op(eng, fn, reads=rd, writes=[out])

    def copy(self, eng, out, in_):
        if eng == "act":
            def fn(e, o=out.ap, i=in_.ap):
                return e.activation(out=o, in_=i, func=AF.Copy)
        else:
            def fn(e, o=out.ap, i=in_.ap):
                return e.tensor_copy(out=o, in_=i)
        self.op(eng, fn, reads=[in_], writes=[out])

    def memset(self, eng, out, val):
        def fn(e, o=out.ap):
            return e.memset(o, val)
        self.op(eng, fn, writes=[out])

    def recip(self, out, in_):
        def fn(e, o=out.ap, i=in_.ap):
            return e.reciprocal(out=o, in_=i)
        self.op("dve", fn, reads=[in_], writes=[out])

    def reduce(self, out, in_, op, axis=None):
        def fn(e, o=out.ap, i=in_.ap):
            return e.tensor_reduce(out=o, in_=i, axis=axis or AX.X, op=op)
        self.op("dve", fn, reads=[in_], writes=[out])

    def _flatten(self, lst, out):
        for x in lst:
            if isinstance(x, list):
                self._flatten(x, out)
            else:
                out.append(x)

    def _analyze(self, flat, sems):
        cnt = {k: 0 for k in sems}
        last_w = {}
        readers = {}
        waited = {e: {} for e in self.ENGS}
        streams = {e: [] for e in self.ENGS}
        pending = []
        trace = []
        import os as _os2
        SERIAL = not bool(_os2.environ.get('NOSERIAL'))
        prev_tok = None
        for op in flat:
            deps = {}
            for k in op.reads:
                lw = last_w.get(k)
                if lw is not None:
                    if deps.get(lw[0], 0) < lw[1]:
                        deps[lw[0]] = lw[1]
            for k in op.writes:
                lw = last_w.get(k)
                if lw is not None:
                    if deps.get(lw[0], 0) < lw[1]:
                        deps[lw[0]] = lw[1]
                rs = readers.get(k)
                if rs:
                    for s, v in rs.items():
                        if deps.get(s, 0) < v:
                            deps[s] = v
            if SERIAL and prev_tok is not None and prev_tok[1] <= cnt[prev_tok[0]]:
                if deps.get(prev_tok[0], 0) < prev_tok[1]:
                    deps[prev_tok[0]] = prev_tok[1]
            waits = []
            wd = waited[op.eng]
            for s, v in deps.items():
                if s == op.eng and (s == "pe" or v > cnt[s]):
                    continue
                if s == "pe" and v > cnt["pe"]:
                    for p in pending:
                        p.inc = True
                    return None
                if wd.get(s, 0) < v:
                    wd[s] = v
                    waits.append((s, v))
            if op.dma:
                sname = "d%d" % op.dsem
                cnt[sname] += 16
                tok = (sname, cnt[sname])
                incinfo = (sname, 16)
            else:
                sname = op.eng
                if op.inc:
                    cnt[sname] += 1
                    tok = (sname, cnt[sname])
                    incinfo = (sname, 1)
                    if sname == "pe":
                        pending = []
                else:
                    tok = (sname, cnt[sname] + 1)
                    incinfo = None
                    if sname == "pe":
                        pending.append(op)
            for k in op.reads:
                readers.setdefault(k, {})[tok[0]] = max(readers.get(k, {}).get(tok[0], 0), tok[1])
            for k in op.writes:
                last_w[k] = tok
                readers[k] = {}
            prev_tok = tok
            streams[op.eng].append((waits, op.fn, incinfo))
            trace.append((op.eng, waits, incinfo, tok, op.fn.__qualname__.split('.')[1] if '.' in op.fn.__qualname__ else op.fn.__qualname__))
        self.trace = trace
        return cnt, streams

    def emit(self, sems, block, final_waits_engine="sp"):
        flat = []
        self._flatten(self.ops, flat)
        while True:
            res = self._analyze(flat, sems)
            if res is not None:
                break
        cnt, streams = res
        self.final_counts = dict(cnt)
        self.n_ops = {e: len(streams[e]) for e in self.ENGS}

        def run(e, lst, is_last):
            for waits, fn, incinfo in lst:
                for s, v in waits:
                    e.wait_ge(sems[s], v)
                ins = fn(e)
                if incinfo is not None:
                    ins.then_inc(sems[incinfo[0]], incinfo[1])
            if is_last:
                for s, v in cnt.items():
                    if v > 0:
                        e.wait_ge(sems[s], v)

        @block.tensor
        def _(e):
            run(e, streams["pe"], False)

        @block.scalar
        def _(e):
            run(e, streams["act"], False)

        @block.vector
        def _(e):
            run(e, streams["dve"], False)

        @block.gpsimd
        def _(e):
            run(e, streams["pool"], False)

        @block.sync
        def _(e):
            run(e, streams["sp"], True)

L = 2
D = 1024
SEQ = 2048
TP = 256
NS_ = 16
DFF = 2816
NKF = 22
C_V, C_GB, C_GC, C_Z, C_XS, C_B, C_C, C_DT, C_Q, C_K, C_VV, C_GA, C_GBB, C_GCC = (
    0, 1024, 2048, 3072, 5120, 7168, 8192, 9216, 9248, 10272, 10528, 10784, 11808, 12832)
NEG = -30000.0
EPS = 1e-6

PC = {}
_o = 0
for _n, _w in [("g_mix_pre", 8), ("g_mix_post", 8), ("g_x_pre", 8), ("g_x_post", 8), ("g_ffn_pre", 8),
               ("g_ffn_post", 8), ("g_mem", 8), ("conv_a_w", 24), ("ssm_conv_w", 128), ("ssm_conv_b", 32),
               ("ssm_norm", 16), ("ssm_d", 16), ("ffn_conv_w", 66), ("ffn_conv_b", 22), ("dtb", 1), ("alog", 1),
               ("sinks_bc", 16), ("alog_bc", 32), ("dtb_bc", 32)]:
    PC[_n] = _o
    _o += _w
NPC = _o
PC_N = {"conv_a_w": 8, "ssm_conv_w": 32, "ffn_conv_w": 22}
CS = {"ident": 0, "U": 128, "ones": 256, "maskOwn": 384, "maskPrev": 896, "eexp": 1408}
NCS = 1408
CB = {"ident": 0, "onesD": 128, "ones2D": 256, "ones": 384, "negmask4": 512}
NCB = 1152


def myhead(c, half):
    return (c // 4) * 8 + half * 4 + (c % 4)


class _Stop(Exception):
    pass


def build_program(debug_tiles=None, debug_layers=None, stop=None):
    from contextlib import ExitStack
    nc = bass.Bass("TRN2", target_bir_lowering=False)

    def din(name, shape, dt=F32):
        return nc.dram_tensor(name, list(shape), dt, kind="ExternalInput").ap()

    def dout(name, shape, dt=F32):
        return nc.dram_tensor(name, list(shape), dt, kind="ExternalOutput").ap()

    xp = din("xp", [D, SEQ]); xs_in = din("xs", [D, NS_]); memT = din("memT", [D, 256])
    W = {}
    for n, shp in [("w_in", [L, D, 13856]), ("wqk", [L, D, 2560]), ("w_a_out", [L, D, D]),
                   ("w_ssm_out", [L, 2048, D]), ("wao", [L, D, D]), ("w_out", [L, D, D]),
                   ("w_xq", [L, D, D]), ("w_xk", [L, D, D]), ("w_xv", [L, D, D]), ("w_xo", [L, D, D]),
                   ("w_ffn_in", [L, D, 2 * DFF]), ("w_ffn_out", [L, DFF, D])]:
        W[n] = din(n, shp)
    pcol_d = din("pcol", [L, 128, NPC]); cst_d = din("cst", [128, NCS]); cstb_d = din("cstb", [128, NCB], BF16)
    ropeP_d = din("ropeP", [128, 2, SEQ]); ropeS_d = din("ropeS", [128, 2, NS_])
    st_ca = din("st_ca", [L, 128, 8, 2 * NS_]); st_sc = din("st_sc", [L, 128, 32, 3 * NS_])
    st_ff = din("st_ff", [L, 128, NKF, 2 * NS_]); st_ssm = din("st_ssm", [L, NS_, 2048, 128])
    c_sk = din("c_sk", [L, NS_, 128, 256]); c_sv = din("c_sv", [L, NS_, 128, 256])
    c_mk = din("c_mk", [L, NS_, 256, 1024]); c_mv = din("c_mv", [L, NS_, 256, 1024])

    yp = dout("yp", [D, SEQ]); ys = dout("ys", [D, NS_])
    o_pca = dout("o_pca", [L, 128, 8, 2]); o_psc = dout("o_psc", [L, 128, 32, 3]); o_pssm = dout("o_pssm", [L, 128, 2048])
    o_pk = dout("o_pk", [L, 2, 128, 128]); o_pv = dout("o_pv", [L, 128, 256]); o_pffn = dout("o_pffn", [L, 128, NKF, 2])
    o_pmk = dout("o_pmk", [L, 8, 128, 256]); o_pmv = dout("o_pmv", [L, 256, 1024])
    o_sca = dout("o_sca", [L, 128, 8, 2 * NS_]); o_ssc = dout("o_ssc", [L, 128, 32, 3 * NS_])
    o_sssm = dout("o_sssm", [L, NS_, 2048, 128]); o_sk = dout("o_sk", [L, NS_, 128, 256]); o_sv = dout("o_sv", [L, NS_, 128, 256])
    o_sffn = dout("o_sffn", [L, 128, NKF, 2 * NS_])

    ARENA = 190 * 1024
    P = Prog(nc, ARENA)
    with ExitStack() as es:
        P.arena = es.enter_context(nc.sbuf_tensor("arena", [128, ARENA // 2], BF16))
        ps = [V(es.enter_context(nc.psum_tensor("ps%d" % i, [128, 512], F32))[:, :], [("ps", i)]) for i in range(7)]
        psb = V(es.enter_context(nc.psum_tensor("psb", [128, 1024], BF16))[:, :], [("ps", 7)])
        semn = ["pe", "act", "dve", "pool"] + ["d%d" % i for i in range(6)]
        sems = {k: es.enter_context(nc.semaphore(k)) for k in semn}
        block = es.enter_context(nc.Block())

        cst = P.alloc([NCS], F32); cstb = P.alloc([NCB], BF16)
        pcol = [P.alloc([NPC], F32) for _ in range(L)]
        P.dma("sp", cst.ap, cst_d, writes=[cst], dsem=1)
        P.dma("sp", cstb.ap, cstb_d, writes=[cstb], dsem=1)
        for l in range(L):
            P.dma("sp", pcol[l].ap, pcol_d[l], writes=[pcol[l]], dsem=1)
        identF = cst[:, 0:128]; UF = cst[:, 128:256]; onesF = cst[:, 256:384]
        maskOwn = cst[:, 384:896]; maskPrev = cst[:, 896:1408]
        identB = cstb[:, 0:128]; onesD = cstb[:, 128:256]; ones2D = cstb[:, 256:384]; onesB = cstb[:, 384:512]
        negmask4 = cstb[:, 512:1024]
        UB = cstb[:, 1024:1152]
        epsc = P.alloc([1], F32)
        P.memset("dve", epsc, EPS)

        def pc(l, name, i, n=1):
            o = PC[name] + i
            return pcol[l][:, o:o + n]

        NSTG, NRING, PF = 1, 2, 1
        stage = [P.alloc([8, 512], F32) for _ in range(NSTG)]
        ring = [P.alloc([8, 512], BF16) for _ in range(NRING)]
        ws = {"i": 0, "slots": [], "init": P.slot()}

        def wslot(k):
            return ws["init"] if k < 0 else ws["slots"][k]

        def wget(name, l, row0, nkc, segs):
            i = ws["i"]; ws["i"] += 1
            st = stage[i % NSTG]; rg = ring[i % NRING]
            src = W[name][l]
            old = P.cur
            P.cur = wslot(i - PF - NSTG)
            off = 0
            for (c0, n) in segs:
                P.dma("sp", st.ap[:, 0:nkc, off:off + n],
                      src[row0:row0 + nkc * 128, c0:c0 + n].rearrange("(kc p) n -> p kc n", p=128),
                      writes=[st], dsem=0)
                off += n
            P.cur = wslot(i - PF)
            P.copy("act" if (i % 2 == 0) else "dve", rg[:, 0:nkc, 0:off], st[:, 0:nkc, 0:off])
            P.cur = old
            ws["slots"].append(P.slot())
            return rg

        bankc = {"i": 0}

        def bank():
            b = ps[bankc["i"] % 6]
            bankc["i"] += 1
            return b

        xT = [P.alloc([TP], F32) for _ in range(8)]
        hb = [P.alloc([TP], BF16) for _ in range(8)]
        macc = [P.alloc([TP], F32) for _ in range(8)]
        a8 = [P.alloc([TP], BF16) for _ in range(8)]
        _o0 = P.arena_off
        ub = [P.alloc([TP], BF16) for _ in range(NKF)]
        ub_all = V(P.arena[:, _o0 // 2:_o0 // 2 + 16 * TP].rearrange("p (j t) -> p j t", j=16), [k for u in ub[0:16] for k in u.keys])
        sq = [P.alloc([TP], BF16) for _ in range(2)]
        rstd = P.alloc([TP], F32); rt = P.alloc([TP], F32)
        tmpF = [P.alloc([512], F32) for _ in range(3)]
        gsb = [P.alloc([TP], BF16) for _ in range(2)]
        ext = [P.alloc([48 + TP], F32) for _ in range(2)]
        haloA = [P.alloc([8, 2], F32) for _ in range(L)]
        haloS = [P.alloc([32, 3], F32) for _ in range(L)]
        haloF = [P.alloc([NKF, 2], F32) for _ in range(L)]
        for l in range(L):
            P.memset("dve", haloA[l], 0.0); P.memset("dve", haloS[l], 0.0); P.memset("dve", haloF[l], 0.0)
        _o1 = P.arena_off
        xsf = [P.alloc([TP], BF16) for _ in range(16)]
        xsf_all = V(P.arena[:, _o1 // 2:_o1 // 2 + 16 * TP].rearrange("p (j t) -> p j t", j=16), [k for u in xsf for k in u.keys])
        Bf = [P.alloc([TP], BF16) for _ in range(8)]
        Cf = [P.alloc([TP], BF16) for _ in range(8)]
        hT = [P.alloc([2048], F32) for _ in range(L)]
        hTb = P.alloc([2048], BF16)
        for l in range(L):
            P.memset("dve", hT[l], 0.0)
        xdt_tm = P.alloc([2048], BF16); xw_tm = P.alloc([2048], BF16); B_tm = P.alloc([1024], BF16)
        a_hi = P.alloc([32], BF16); a_lo = P.alloc([32], BF16); a_res = P.alloc([32], F32)
        dt_tm = P.alloc([32], F32); a_tm = P.alloc([32], F32); cs_tm = P.alloc([32], F32)
        dcs_tm = P.alloc([32], F32); ecl_bc = P.alloc([32], F32); A_bc = [P.alloc([32], F32) for _ in range(L)]
        ecsb = P.alloc([4, 128], BF16); CH = [P.alloc([128], BF16) for _ in range(2)]
        argb = P.alloc([4, 128], F32); Eb = argb; MT = [P.alloc([4, 128], BF16) for _ in range(2)]
        qr = [P.alloc([TP], BF16) for _ in range(8)]
        kf = [[P.alloc([128 + TP], BF16) for _ in range(2)] for _ in range(L)]
        vtm = [P.alloc([TP // 128 + 1, 256], BF16) for _ in range(L)]
        kf32 = [P.alloc([128], F32) for _ in range(2)]
        v32 = P.alloc([256], F32)
        smb = P.alloc([512], F32); Ebf = [P.alloc([512], BF16) for _ in range(2)]
        esink = [P.alloc([16], F32) for _ in range(L)]
        memK = [[P.alloc([256], BF16) for _ in range(8)] for _ in range(L)]
        memV = [P.alloc([2, 1024], BF16) for _ in range(L)]
        AcolS = [P.alloc([1], F32) for _ in range(L)]

        for l in range(L):
            P.act(esink[l], pc(l, "sinks_bc", 0, 16), AF.Exp)
            P.act(A_bc[l], pc(l, "alog_bc", 0, 32), AF.Exp)
            P.ts("dve", A_bc[l], A_bc[l], -1.0, ALU.mult)
            P.act(AcolS[l], pc(l, "alog", 0, 1), AF.Exp)
            P.ts("dve", AcolS[l], AcolS[l], -1.0, ALU.mult)

        def rms_stats(src_chunks, T, ones_m):
            n = len(src_chunks)
            for c in range(n):
                s = sq[c % 2]
                P.act(s[:, 0:T], src_chunks[c][:, 0:T], AF.Square)
                P.mm(ps[6][:, 0:T], ones_m, s[:, 0:T], c == 0, c == n - 1)
            P.act(rt[:, 0:T], ps[6][:, 0:T], AF.Sqrt, bias=epsc)
            P.recip(rstd[:, 0:T], rt[:, 0:T])

        def norm_to_bf(src_chunks, l, gname, dst_chunks, T, ones_m=None):
            rms_stats(src_chunks, T, ones_m if ones_m is not None else onesD)
            for c in range(len(src_chunks)):
                P.stt("dve", dst_chunks[c][:, 0:T], src_chunks[c][:, 0:T], pc(l, gname, c), rstd[:, 0:T],
                      ALU.mult, ALU.mult)

        def post_norm_residual(l, gname, T):
            rms_stats(macc, T, onesD)
            for c in range(8):
                t = tmpF[c % 3]
                P.stt("dve", t[:, 0:T], macc[c][:, 0:T], pc(l, gname, c), rstd[:, 0:T], ALU.mult, ALU.mult)
                P.tt("dve", xT[c][:, 0:T], xT[c][:, 0:T], t[:, 0:T], ALU.add)

        def proj_fm(pso, wb, nkc, col0, M, rhs_chunks, T, start=True, stop=True, kc0=0):
            for kc in range(nkc):
                P.mm(pso, wb[:, kc, col0:col0 + M], rhs_chunks[kc0 + kc][:, 0:T],
                     start and kc == 0, stop and kc == nkc - 1)

        def conv_chunk(l, T, sample, src, wname, ntap, cidx, nch, halo, st_d, so_d, po_last):
            e = ext[0]; r = ext[1]
            stride = NS_ if sample else 1
            H = (ntap - 1) * stride
            if sample:
                P.dma("sp", e.ap[:, 0:H], st_d[l, :, cidx, :], writes=[e], dsem=2)
            else:
                P.copy("dve", e[:, 0:H], halo[l][:, cidx, :])
            return e, r, H, stride

        def chk(k):
            if stop is not None and k >= stop:
                raise _Stop()

        def layer(l, T, sample, ti):
            t0 = 0 if sample else ti * TP
            last = (not sample) and ti == SEQ // TP - 1
            nblk = 1 if sample else T // 128

            def conv(src, wname, bname, ntap, cidx, halo, st_d, so_d, out_fn):
                e = ext[0]; r = ext[1]
                stride = NS_ if sample else 1
                H = (ntap - 1) * stride
                if sample:
                    P.dma("sp", e.ap[:, 0:H], st_d[l, :, cidx, :], writes=[e], dsem=2)
                else:
                    P.copy("dve", e[:, 0:H], halo[l][:, cidx, :])
                src(e[:, H:H + T])
                for j in range(ntap):
                    wcol = pc(l, wname, j * (PC_N[wname]) + cidx)
                    if j == 0:
                        P.ts("dve", r[:, 0:T], e[:, 0:T], wcol, ALU.mult)
                    else:
                        P.stt("dve", r[:, 0:T], e[:, j * stride:j * stride + T], wcol, r[:, 0:T], ALU.mult, ALU.add)
                if sample:
                    P.dma("sp", so_d[l, :, cidx, :], e.ap[:, stride:stride + H], reads=[e], dsem=3)
                else:
                    P.copy("dve", halo[l][:, cidx, :], e[:, T:T + H])
                out_fn(r[:, 0:T])

            norm_to_bf(xT, l, "g_mix_pre", hb, T)
            chk(1)

            for c in range(8):
                wb = wget("w_in", l, 0, 8, [(C_V + c * 128, 128), (C_GB + c * 128, 128), (C_GC + c * 128, 128)])
                pv, pgb, pgc = bank(), bank(), bank()
                proj_fm(pv[:, 0:T], wb, 8, 0, 128, hb, T)
                proj_fm(pgb[:, 0:T], wb, 8, 128, 128, hb, T)
                proj_fm(pgc[:, 0:T], wb, 8, 256, 128, hb, T)
                vs = tmpF[0]
                P.copy("act", vs[:, 0:T], pv[:, 0:T])

                def src(dst, pgc=pgc, vs=vs):
                    P.tt("dve", dst, pgc[:, 0:T], vs[:, 0:T], ALU.mult)

                def outf(rv, pgb=pgb, c=c):
                    P.tt("dve", a8[c][:, 0:T], pgb[:, 0:T], rv, ALU.mult)
                conv(src, "conv_a_w", None, 3, c, haloA, st_ca, o_sca, outf)

            def gated_out(wname, gate_col, in_chunks, nk_in, first):
                nkb = (nk_in + 7) // 8
                for half in range(2):
                    pys = [bank() for _ in range(4)]
                    for kb in range(nkb):
                        nk = min(8, nk_in - kb * 8)
                        wb_ = wget(wname, l, kb * 1024, nk, [(half * 512, 512)])
                        for oc in range(4):
                            proj_fm(pys[oc][:, 0:T], wb_, nk, oc * 128, 128, in_chunks, T,
                                    start=(kb == 0), stop=(kb == nkb - 1), kc0=kb * 8)
                    wg = wget("w_in", l, 0, 8, [(gate_col + half * 512, 512)])
                    for oc in range(4):
                        c = half * 4 + oc
                        pg = ps[6] if False else bank()
                        proj_fm(pg[:, 0:T], wg, 8, oc * 128, 128, hb, T)
                        g = gsb[oc % 2]
                        P.act(g[:, 0:T], pg[:, 0:T], AF.Sigmoid)
                        if first:
                            P.tt("dve", macc[c][:, 0:T], pys[oc][:, 0:T], g[:, 0:T], ALU.mult)
                        else:
                            t = tmpF[oc % 3]
                            P.tt("dve", t[:, 0:T], pys[oc][:, 0:T], g[:, 0:T], ALU.mult)
                            P.tt("dve", macc[c][:, 0:T], macc[c][:, 0:T], t[:, 0:T], ALU.add)

            chk(2)
            gated_out("w_a_out", C_GA, a8, 8, True)
            chk(3)

            dsts = xsf + Bf + Cf
            for blk in range(8):
                wb = wget("w_in", l, 0, 8, [(C_XS + blk * 512, 512)])
                for oc in range(4):
                    c = blk * 4 + oc
                    px = bank()
                    proj_fm(px[:, 0:T], wb, 8, oc * 128, 128, hb, T)

                    def src(dst, px=px):
                        P.copy("act", dst, px[:, 0:T])

                    def outf(rv, c=c):
                        P.act(dsts[c][:, 0:T], rv, AF.Silu, bias=pc(l, "ssm_conv_b", c))
                    conv(src, "ssm_conv_w", None, 4, c, haloS, st_sc, o_ssc, outf)
            chk(4)
            wdt = wget("w_in", l, 0, 8, [(C_DT, 32)])
            if not sample:
                for tc in range(T // 128):
                    tk = slice(tc * 128, (tc + 1) * 128)
                    pd = bank()
                    for kc in range(8):
                        P.mm(pd[:, 0:32], hb[kc][:, tk], wdt[:, kc, 0:32], kc == 0, kc == 7)
                    P.tt("dve", dt_tm, pd[:, 0:32], pc(l, "dtb_bc", 0, 32), ALU.add)
                    P.act(dt_tm, dt_tm, AF.Exp)
                    P.act(dt_tm, dt_tm, AF.Ln, bias=1.0)
                    P.tt("dve", a_tm, dt_tm, A_bc[l], ALU.mult)
                    P.copy("dve", a_hi, a_tm)
                    P.tt("dve", a_res, a_tm, a_hi, ALU.subtract)
                    P.copy("dve", a_lo, a_res)
                    chk(4.1)
                    pcs = bank()
                    P.mm(pcs[:, 0:32], UF, a_tm, True, True)
                    P.mm(pcs[:, 32:64], onesF, a_tm, True, True)
                    P.copy("dve", cs_tm, pcs[:, 0:32])
                    P.tt("dve", dcs_tm, pcs[:, 32:64], cs_tm, ALU.subtract)
                    P.act(dcs_tm, dcs_tm, AF.Exp)
                    P.act(ecl_bc, pcs[:, 32:64], AF.Exp)
                    chk(4.2)
                    for hf in range(2):
                        for j in range(8):
                            P.transpose(psb[:, j * 128:(j + 1) * 128], xsf[hf * 8 + j][:, tk], identB, inc=(j == 7))
                        xv = V(psb.ap.rearrange("p (h d) -> p h d", h=16), psb.keys)
                        dv = V(dt_tm.ap[:, hf * 16:(hf + 1) * 16].unsqueeze(2).to_broadcast([128, 16, 64]), dt_tm.keys)
                        ov = V(xdt_tm.ap[:, hf * 1024:(hf + 1) * 1024].rearrange("p (h d) -> p h d", h=16), xdt_tm.keys)
                        P.tt("dve", ov, xv, dv, ALU.mult)
                    xv = V(xdt_tm.ap.rearrange("p (h d) -> p h d", h=32), xdt_tm.keys)
                    dv = V(dcs_tm.ap.unsqueeze(2).to_broadcast([128, 32, 64]), dcs_tm.keys)
                    ov = V(xw_tm.ap.rearrange("p (h d) -> p h d", h=32), xw_tm.keys)
                    P.tt("dve", ov, xv, dv, ALU.mult)
                    chk(4.3)
                    for g in range(8):
                        P.transpose(psb[:, g * 128:(g + 1) * 128], Bf[g][:, tk], identB, inc=(g == 7))
                    P.copy("act", B_tm, psb)
                    P.copy("act", hTb, hT[l])
                    chk(4.4)
                    import os as _os
                    for g in [int(x) for x in _os.environ.get('GLIST', '0,1,2,3,4,5,6,7').split(',')]:
                        pcb = bank()
                        P.mm(pcb[:, 0:128], Bf[g][:, tk], Cf[g][:, tk], True, True)
                        pb = bank()
                        for r in range(4):
                            h = 4 * g + r
                            P.mm(pb[:, r * 128:(r + 1) * 128],
                                 V(a_hi.ap[:, h:h + 1].to_broadcast([128, 128]), a_hi.keys), UB,
                                 True, False, inc=False)
                            P.mm(pb[:, r * 128:(r + 1) * 128],
                                 V(a_lo.ap[:, h:h + 1].to_broadcast([128, 128]), a_lo.keys), UB,
                                 False, True, inc=(r == 3))
                        pb3 = V(pb.ap.rearrange("p (r l) -> p r l", r=4), pb.keys)
                        P.act(ecsb, pb3, AF.Exp)
                        cb = V(cs_tm.ap[:, 4 * g:4 * g + 4].unsqueeze(2).to_broadcast([128, 4, 128]), cs_tm.keys)
                        P.tt("dve", argb, pb3, cb, ALU.subtract)
                        P.tt("dve", argb, argb, V(negmask4.ap.rearrange("p (r l) -> p r l", r=4), negmask4.keys), ALU.add)
                        P.act(Eb, argb, AF.Exp)
                        mt = MT[g % 2]
                        cbv = V(pcb.ap[:, 0:128].unsqueeze(1).to_broadcast([128, 4, 128]), pcb.keys)
                        P.tt("dve", mt, Eb, cbv, ALU.mult)
                        chk(4.5)
                        chk(4.7 + g * 0.01 + 0.001)
                        for jj in ([] if _os.environ.get('SKIPY') else range(2)):
                            j = 2 * g + jj
                            py = bank()
                            for hl in range(2):
                                r = 2 * jj + hl
                                h = 4 * g + r
                                ch = CH[hl]
                                if not _os.environ.get('E2TEST'):
                                    P.tt("dve", ch, Cf[g][:, tk], ecsb[:, r, :], ALU.mult)
                                if _os.environ.get('E2TEST'):
                                    P.mm(py[:, hl * 128:(hl + 1) * 128], xdt_tm[:, j * 128:(j + 1) * 128], mt[:, r, :],
                                         True, True, inc=(hl == 1))
                                    continue
                                if _os.environ.get('M128TEST'):
                                    P.mm(py[:, hl * 128:(hl + 1) * 128], xdt_tm[:, j * 128:(j + 1) * 128], mt[:, r, :],
                                         True, False, inc=False)
                                    P.mm(py[:, hl * 128:(hl + 1) * 128], hTb[:, j * 128:(j + 1) * 128], ch,
                                         False, True, inc=(hl == 1))
                                    continue
                                if _os.environ.get('OFFTEST'):
                                    P.mm(py[0:64, hl * 128:(hl + 1) * 128], xdt_tm[:, h * 64:(h + 1) * 64], mt[:, r, :],
                                         True, False, inc=False)
                                    P.mm(py[0:64, hl * 128:(hl + 1) * 128], hTb[:, h * 64:(h + 1) * 64], ch,
                                         False, True, inc=(hl == 1))
                                    continue
                                P.mm(py[hl * 64:(hl + 1) * 64, 0:128], xdt_tm[:, h * 64:(h + 1) * 64], mt[:, r, :],
                                     True, False, inc=False)
                                P.mm(py[hl * 64:(hl + 1) * 64, 0:128], hTb[:, h * 64:(h + 1) * 64], ch,
                                     False, True, inc=(hl == 1))
                            if not _os.environ.get('E1TEST'):
                                P.stt("dve", ub[j][:, tk], xsf[j][:, tk], pc(l, "ssm_d", j), py[:, 0:128], ALU.mult, ALU.add)
                            chk(4.6)
                            chk(4.7 + g * 0.01 + 0.002 + 0.001 * jj)
                        if _os.environ.get('SKIPST'):
                            continue
                        pst = bank()
                        P.mm(pst[:, 0:256], B_tm[:, g * 128:(g + 1) * 128], xw_tm[:, g * 256:(g + 1) * 256], True, True)
                        hv = V(hT[l].ap[:, g * 256:(g + 1) * 256].rearrange("p (r d) -> p r d", r=4), hT[l].keys)
                        ev = V(ecl_bc.ap[:, 4 * g:4 * g + 4].unsqueeze(2).to_broadcast([128, 4, 64]), ecl_bc.keys)
                        P.tt("dve", hv, hv, ev, ALU.mult)
                        P.tt("dve", hT[l][:, g * 256:(g + 1) * 256], hT[l][:, g * 256:(g + 1) * 256], pst[:, 0:256], ALU.add)
                        chk(4.7)
                        chk(4.7 + g * 0.01 + 0.004)
                    chk(4.8)
                if last:
                    P.dma("sp", o_pssm[l], hT[l].ap, reads=[hT[l]], dsem=3)
                    P.dma("sp", o_pca[l], haloA[l].ap, reads=[haloA[l]], dsem=3)
                    P.dma("sp", o_psc[l], haloS[l].ap, reads=[haloS[l]], dsem=3)
            else:
                ssd_sample(l, wdt)
            chk(5)
            for blk in range(4):
                wb = wget("w_in", l, 0, 8, [(C_Z + blk * 512, 512)])
                for oc in range(4):
                    j = blk * 4 + oc
                    pz = bank()
                    proj_fm(pz[:, 0:T], wb, 8, oc * 128, 128, hb, T)
                    g = gsb[oc % 2]
                    P.act(g[:, 0:T], pz[:, 0:T], AF.Silu)
                    P.tt("dve", ub[j][:, 0:T], ub[j][:, 0:T], g[:, 0:T], ALU.mult)
            rms_stats(ub[0:16], T, ones2D)
            for j in range(16):
                P.stt("dve", ub[j][:, 0:T], ub[j][:, 0:T], pc(l, "ssm_norm", j), rstd[:, 0:T], ALU.mult, ALU.mult)
            gated_out("w_ssm_out", C_GBB, ub, 16, False)
            chk(6)

            rope_d = ropeS_d if sample else ropeP_d
            cosb = tmpF[1]; sinb = tmpF[2]
            P.dma("sp", cosb.ap[:, 0:T], rope_d[:, 0, t0:t0 + T], writes=[cosb], dsem=2)
            P.dma("sp", sinb.ap[:, 0:T], rope_d[:, 1, t0:t0 + T], writes=[sinb], dsem=2)
            for blk in range(5):
                wb = wget("wqk", l, 0, 8, [(blk * 512, 512)])
                for i2 in range(2):
                    c = blk * 2 + i2
                    pq = bank(); pp = bank()
                    proj_fm(pq[:, 0:T], wb, 8, i2 * 256, 128, hb, T)
                    proj_fm(pp[:, 0:T], wb, 8, i2 * 256 + 128, 128, hb, T)
                    t1 = tmpF[0]
                    P.tt("dve", t1[:, 0:T], pq[:, 0:T], cosb[:, 0:T], ALU.mult)
                    t2 = smb
                    P.tt("dve", t2[:, 0:T], pp[:, 0:T], sinb[:, 0:T], ALU.mult)
                    if c < 8:
                        P.tt("dve", qr[c][:, 0:T], t1[:, 0:T], t2[:, 0:T], ALU.add)
                    else:
                        kc_ = c - 8
                        if sample:
                            P.tt("dve", knew[kc_], t1[:, 0:T], t2[:, 0:T], ALU.add)
                        else:
                            P.tt("dve", kf[l][kc_][:, 128:128 + T], t1[:, 0:T], t2[:, 0:T], ALU.add)
                            if last:
                                P.tt("dve", kf32[kc_], t1[:, T - 128:T], t2[:, T - 128:T], ALU.add)
                                P.dma("sp", o_pk[l, kc_], kf32[kc_].ap, reads=[kf32[kc_]], dsem=3)
            wv = wget("w_in", l, 0, 8, [(C_VV, 256)])
            if not sample:
                for qb in range(T // 128):
                    pvv = bank()
                    for kc in range(8):
                        P.mm(pvv[:, 0:256], hb[kc][:, qb * 128:(qb + 1) * 128], wv[:, kc, 0:256], kc == 0, kc == 7)
                    P.copy("act", vtm[l][:, qb + 1, :], pvv[:, 0:256])
                    if last and qb == T // 128 - 1:
                        P.copy("dve", v32, pvv[:, 0:256])
                        P.dma("sp", o_pv[l], v32.ap, reads=[v32], dsem=3)
                for qb in range(T // 128):
                    first_blk = (ti == 0 and qb == 0)
                    kbs = [1] if first_blk else [0, 1]
                    for g in range(4):
                        kc_ = g // 2; hf = g % 2
                        rows = slice(hf * 64, hf * 64 + 64)
                        cbase = (g // 2) * 4
                        Es = []
                        for kb in kbs:
                            pS = bank()
                            for r in range(4):
                                P.mm(pS[:, r * 128:(r + 1) * 128],
                                     kf[l][kc_][rows, (qb + kb) * 128:(qb + kb + 1) * 128],
                                     qr[cbase + r][rows, qb * 128:(qb + 1) * 128], True, True, inc=(r == 3))
                            P.stt("dve", smb, pS, 0.125, maskOwn if kb == 1 else maskPrev, ALU.mult, ALU.add)
                            E = Ebf[kb]
                            P.act(E, smb, AF.Exp)
                            Es.append((kb, E))
                        po = bank(); pl = bank()
                        for i, (kb, E) in enumerate(Es):
                            P.mm(po[rows, :], vtm[l][:, qb + kb, g * 64:(g + 1) * 64], E, i == 0, i == len(Es) - 1)
                        for i, (kb, E) in enumerate(Es):
                            P.mm(pl[rows, :], onesB[:, 0:64], E, i == 0, i == len(Es) - 1)
                        den = tmpF[0]
                        P.tt("dve", V(den.ap[rows, :].rearrange("p (r q) -> p r q", r=4), den.keys),
                             V(pl.ap[rows, :].rearrange("p (r q) -> p r q", r=4), pl.keys),
                             V(esink[l].ap[rows, 4 * g:4 * g + 4].unsqueeze(2).to_broadcast([64, 4, 128]), esink[l].keys),
                             ALU.add)
                        P.recip(den[rows, :], den[rows, :])
                        for r in range(4):
                            P.tt("dve", a8[cbase + r][rows, qb * 128:(qb + 1) * 128], po[rows, r * 128:(r + 1) * 128],
                                 den[rows, r * 128:(r + 1) * 128], ALU.mult)
                for kc_ in range(2):
                    P.copy("dve", kf[l][kc_][:, 0:128], kf[l][kc_][:, T:T + 128])
                P.copy("dve", vtm[l][:, 0, :], vtm[l][:, T // 128, :])
            else:
                swa_sample(l, wv)
            chk(7)
            gated_out("wao", C_GCC, a8, 8, False)
            chk(8)

            def plain_out(wname, in_chunks, nk_in, rows_per_blk=8):
                nkb = (nk_in + 7) // 8
                for half in range(2):
                    pys = [bank() for _ in range(4)]
                    for kb in range(nkb):
                        nk = min(8, nk_in - kb * 8)
                        wb_ = wget(wname, l, kb * 1024, nk, [(half * 512, 512)])
                        for oc in range(4):
                            proj_fm(pys[oc][:, 0:T], wb_, nk, oc * 128, 128, in_chunks, T,
                                    start=(kb == 0), stop=(kb == nkb - 1), kc0=kb * 8)
                    for oc in range(4):
                        P.copy("act", macc[half * 4 + oc][:, 0:T], pys[oc][:, 0:T])

            for c in range(8):
                P.copy("act", a8[c][:, 0:T], macc[c][:, 0:T])
            plain_out("w_out", a8, 8)
            chk(8.5)
            post_norm_residual(l, "g_mix_post", T)
            chk(9)

            if (not sample) and ti == 0:
                mem_kv(l)
            norm_to_bf(xT, l, "g_x_pre", hb, T)
            for half in range(2):
                wb = wget("w_xq", l, 0, 8, [(half * 512, 512)])
                for oc in range(4):
                    pq = bank()
                    proj_fm(pq[:, 0:T], wb, 8, oc * 128, 128, hb, T)
                    P.copy("act", qr[half * 4 + oc][:, 0:T], pq[:, 0:T])
            if not sample:
                cross_core(memK[l], memV[l], 0, T, 0)
            else:
                cross_sample(l)
            plain_out("w_xo", a8, 8)
            post_norm_residual(l, "g_x_post", T)
            chk(10)

            norm_to_bf(xT, l, "g_ffn_pre", hb, T)
            for blk in range(6):
                ncol = 512 if blk < 5 else 256
                nch = ncol // 128
                wa = wget("w_ffn_in", l, 0, 8, [(blk * 512, ncol)])
                pas = [bank() for _ in range(nch)]
                for oc in range(nch):
                    proj_fm(pas[oc][:, 0:T], wa, 8, oc * 128, 128, hb, T)
                wg = wget("w_ffn_in", l, 0, 8, [(DFF + blk * 512, ncol)])
                for oc in range(nch):
                    c = blk * 4 + oc
                    pa = pas[oc]; pg = bank()
                    proj_fm(pg[:, 0:T], wg, 8, oc * 128, 128, hb, T)

                    def src(dst, pa=pa):
                        P.copy("act", dst, pa[:, 0:T])

                    def outf(rv, c=c, pg=pg):
                        g = gsb[c % 2]
                        P.act(g[:, 0:T], rv, AF.Silu, bias=pc(l, "ffn_conv_b", c))
                        P.tt("dve", ub[c][:, 0:T], pg[:, 0:T], g[:, 0:T], ALU.mult)
                    conv(src, "ffn_conv_w", None, 3, c, haloF, st_ff, o_sffn, outf)
            if last:
                P.dma("sp", o_pffn[l], haloF[l].ap, reads=[haloF[l]], dsem=3)
            plain_out("w_ffn_out", ub, NKF)
            post_norm_residual(l, "g_ffn_post", T)

        def cross_core(mK, mV, qcol0, N, ocol0):
            for hx in range(4):
                Es = []
                for kb in range(2):
                    pS = bank()
                    for dc in range(2):
                        P.mm(pS[:, 0:N], mK[2 * hx + dc][:, kb * 128:(kb + 1) * 128],
                             qr[2 * hx + dc][:, qcol0:qcol0 + N], dc == 0, dc == 1)
                    E = Ebf[kb]
                    P.act(E[:, 0:N], pS[:, 0:N], AF.Exp, scale=1.0 / 16.0)
                    Es.append(E)
                pl = bank()
                for kb in range(2):
                    P.mm(pl[:, 0:N], onesB, Es[kb][:, 0:N], kb == 0, kb == 1)
                den = tmpF[0]
                P.recip(den[:, 0:N], pl[:, 0:N])
                for dc in range(2):
                    po = bank()
                    for kb in range(2):
                        P.mm(po[:, 0:N], mV[:, kb, hx * 256 + dc * 128:hx * 256 + (dc + 1) * 128], Es[kb][:, 0:N],
                             kb == 0, kb == 1)
                    P.tt("dve", a8[2 * hx + dc][:, ocol0:ocol0 + N], po[:, 0:N], den[:, 0:N], ALU.mult)

        def mem_kv(l):
            mt = [ub[c] for c in range(8)]
            for c in range(8):
                P.dma("sp", macc[c].ap[:, 0:256], memT[c * 128:(c + 1) * 128, :], writes=[macc[c]], dsem=2)
            mn = [ub[8 + c] for c in range(8)]
            norm_to_bf(macc, l, "g_mem", mn, 256)
            for half in range(2):
                wb = wget("w_xk", l, 0, 8, [(half * 512, 512)])
                for oc in range(4):
                    c = half * 4 + oc
                    pk = bank()
                    proj_fm(pk[:, 0:256], wb, 8, oc * 128, 128, mn, 256)
                    P.copy("act", memK[l][c], pk[:, 0:256])
                    t = tmpF[c % 3]
                    P.copy("dve", t[:, 0:256], pk[:, 0:256])
                    P.dma("sp", o_pmk[l, c], t.ap[:, 0:256], reads=[t], dsem=3)
            for half in range(2):
                wb = wget("w_xv", l, 0, 8, [(half * 512, 512)])
                for kb in range(2):
                    pv_ = bank()
                    for kc in range(8):
                        P.mm(pv_, mn[kc][:, kb * 128:(kb + 1) * 128], wb[:, kc, 0:512], kc == 0, kc == 7)
                    P.copy("act", memV[l][:, kb, half * 512:(half + 1) * 512], pv_)
                    t = tmpF[(kb + 2 * half) % 3]
                    P.copy("dve", t, pv_)
                    P.dma("sp", o_pmv[l, kb * 128:(kb + 1) * 128, half * 512:(half + 1) * 512], t.ap, reads=[t], dsem=3)

        knew = [P.alloc([NS_], F32) for _ in range(2)]

        dtf = P.alloc([NS_], F32); decf = P.alloc([NS_], F32)
        dec_exp = P.alloc([16, NS_], F32); xdt_exp = P.alloc([16, NS_], F32); ysall = P.alloc([16, NS_], F32)
        ysm = P.alloc([8], F32)
        knew_tm = P.alloc([256], F32); vnew_tm = P.alloc([256], F32)
        ktm32 = P.alloc([256], F32); vtm32 = P.alloc([256], F32)
        kb16 = P.alloc([256], BF16); vb16 = P.alloc([256], BF16); kbT = P.alloc([2, 128], BF16)
        Esm = P.alloc([16], BF16); osm = P.alloc([NS_, 16], F32); dsm = P.alloc([NS_, 16], F32)

        def ssd_sample(l, wdt):
            pd = bank()
            for kc in range(8):
                P.mm(pd[0:32, 0:NS_], wdt[:, kc, 0:32], hb[kc][:, 0:NS_], kc == 0, kc == 7)
            P.act(dtf[0:32, :], pd[0:32, 0:NS_], AF.Exp, bias=pc(l, "dtb", 0)[0:32, :])
            P.act(dtf[0:32, :], dtf[0:32, :], AF.Ln, bias=1.0)
            P.ts("dve", decf[0:32, :], dtf[0:32, :], AcolS[l][0:32, :], ALU.mult)
            P.act(decf[0:32, :], decf[0:32, :], AF.Exp)
            pe_ = bank()
            for j in range(16):
                for hl in range(2):
                    lh = V(identF.ap[0:32, 2 * j + hl:2 * j + hl + 1].to_broadcast([32, 64]), identF.keys)
                    rws = slice(hl * 64, hl * 64 + 64)
                    P.mm(pe_[rws, j * 16:(j + 1) * 16], lh, decf[0:32, :], True, True, inc=False)
                    P.mm(pe_[rws, 256 + j * 16:256 + (j + 1) * 16], lh, dtf[0:32, :], True, True, inc=(j == 15 and hl == 1))
            P.copy("dve", dec_exp, V(pe_.ap[:, 0:256].rearrange("p (j b) -> p j b", j=16), pe_.keys))
            P.tt("dve", xdt_exp, xsf_all[:, :, 0:NS_], V(pe_.ap[:, 256:512].rearrange("p (j b) -> p j b", j=16), pe_.keys), ALU.mult)
            S = V(hT[0].ap.rearrange("p (j n) -> p j n", j=16), hT[0].keys)
            tmp = V(hTb.ap.bitcast(F32).rearrange("p (j n) -> p j n", j=8), hTb.keys)
            for b in range(NS_):
                P.dma("sp", S.ap, st_ssm[l, b].rearrange("(j q) n -> q j n", q=128), writes=[S], dsem=4)
                for half in range(2):
                    pB = bank(); pC = bank()
                    for gi in range(4):
                        g = half * 4 + gi
                        P.mm(pB[:, gi * 128:(gi + 1) * 128], V(Bf[g].ap[:, b:b + 1].to_broadcast([128, 128]), Bf[g].keys), identB,
                             True, True, inc=(gi == 3))
                    for gi in range(4):
                        g = half * 4 + gi
                        P.mm(pC[:, gi * 128:(gi + 1) * 128], V(Cf[g].ap[:, b:b + 1].to_broadcast([128, 128]), Cf[g].keys), identB,
                             True, True, inc=(gi == 3))
                    jh = slice(half * 8, half * 8 + 8)
                    Sh = V(S.ap[:, jh, :].rearrange("p (g r) n -> p g r n", r=2), S.keys)
                    tmp4 = V(tmp.ap.rearrange("p (g r) n -> p g r n", r=2), tmp.keys)

                    def bcn(v):
                        return V(v.ap.rearrange("p (g r) -> p g r", r=2).unsqueeze(3).to_broadcast([128, 4, 2, 128]), v.keys)

                    def bcg(pv):
                        return V(pv.ap.rearrange("p (g n) -> p g n", g=4).unsqueeze(2).to_broadcast([128, 4, 2, 128]), pv.keys)
                    P.tt("dve", Sh, Sh, bcn(dec_exp[:, jh, b]), ALU.mult)
                    P.tt("dve", tmp4, bcg(pB), bcn(xdt_exp[:, jh, b]), ALU.mult)
                    P.tt("dve", Sh, Sh, tmp4, ALU.add)
                    P.tt("dve", tmp4, Sh, bcg(pC), ALU.mult)
                    P.reduce(ysall[:, jh, b], tmp, ALU.add)
                P.dma("sp", o_sssm[l, b].rearrange("(j q) n -> q j n", q=128), S.ap, reads=[S], dsem=5)
            dcol = V(pcol[l].ap[:, PC["ssm_d"]:PC["ssm_d"] + 16].unsqueeze(2).to_broadcast([128, 16, NS_]), pcol[l].keys)
            P.tt("dve", xdt_exp, xsf_all[:, :, 0:NS_], dcol, ALU.mult)
            P.tt("dve", ub_all[:, :, 0:NS_], xdt_exp, ysall, ALU.add)

        def swa_sample(l, wv):
            pvv = bank()
            for kc in range(8):
                P.mm(pvv[0:NS_, 0:256], hb[kc][:, 0:NS_], wv[:, kc, 0:256], kc == 0, kc == 7)
            P.copy("dve", vnew_tm[0:NS_, :], pvv[0:NS_, 0:256])
            pk = bank()
            for kc_ in range(2):
                P.transpose(pk[0:NS_, kc_ * 128:(kc_ + 1) * 128], knew[kc_], identF, inc=(kc_ == 1))
            P.copy("dve", knew_tm[0:NS_, :], pk[0:NS_, 0:256])
            po = ps[6][:, 0:256]; pl = ps[6][:, 256:512]
            for b in range(NS_):
                P.dma("sp", ktm32.ap[0:127, :], c_sk[l, b, 1:128, :], writes=[ktm32], dsem=4)
                P.dma("sp", ktm32.ap[127:128, :], knew_tm.ap[b:b + 1, :], reads=[knew_tm], writes=[ktm32], dsem=4)
                P.dma("sp", vtm32.ap[0:127, :], c_sv[l, b, 1:128, :], writes=[vtm32], dsem=4)
                P.dma("sp", vtm32.ap[127:128, :], vnew_tm.ap[b:b + 1, :], reads=[vnew_tm], writes=[vtm32], dsem=4)
                P.dma("sp", o_sk[l, b], ktm32.ap, reads=[ktm32], dsem=5)
                P.dma("sp", o_sv[l, b], vtm32.ap, reads=[vtm32], dsem=5)
                P.copy("dve", kb16, ktm32)
                P.copy("dve", vb16, vtm32)
                for kc_ in range(2):
                    P.transpose(psb[:, kc_ * 128:(kc_ + 1) * 128], kb16[:, kc_ * 128:(kc_ + 1) * 128], identB, inc=(kc_ == 1))
                P.copy("act", kbT, V(psb.ap[:, 0:256].rearrange("p (c k) -> p c k", c=2), psb.keys))
                pS = bank()
                for g in range(4):
                    rows = slice((g % 2) * 64, (g % 2) * 64 + 64)
                    cbase = (g // 2) * 4
                    for r in range(4):
                        P.mm(pS[:, g * 4 + r:g * 4 + r + 1], kbT[rows, g // 2, :], qr[cbase + r][rows, b:b + 1],
                             True, True, inc=(g == 3 and r == 3))
                P.act(Esm, pS[:, 0:16], AF.Exp, scale=0.125)
                for g in range(4):
                    rows = slice((g % 2) * 64, (g % 2) * 64 + 64)
                    P.mm(po[rows, b * 16 + g * 4:b * 16 + g * 4 + 4], vb16[:, g * 64:(g + 1) * 64], Esm[:, g * 4:g * 4 + 4],
                         True, True, inc=False)
                    P.mm(pl[rows, b * 16 + g * 4:b * 16 + g * 4 + 4], onesB[:, 0:64], Esm[:, g * 4:g * 4 + 4],
                         True, True, inc=(g == 3))
            es = V(esink[l].ap.unsqueeze(1).to_broadcast([128, NS_, 16]), esink[l].keys)
            P.tt("dve", dsm, V(pl.ap.rearrange("p (b h) -> p b h", b=NS_), pl.keys), es, ALU.add)
            P.recip(dsm, dsm)
            P.tt("dve", osm, V(po.ap.rearrange("p (b h) -> p b h", b=NS_), po.keys), dsm, ALU.mult)
            for c in range(8):
                for half in range(2):
                    rows = slice(half * 64, half * 64 + 64)
                    h = myhead(c, half)
                    P.copy("dve", a8[c][rows, 0:NS_], osm[rows, :, h])

        def cross_sample(l):
            K32 = V(hT[1].ap.rearrange("p (k d) -> p k d", k=2), hT[1].keys)
            Kb = V(xdt_tm.ap.rearrange("p (k d) -> p k d", k=2), xdt_tm.keys)
            for b in range(NS_):
                P.dma("sp", K32.ap, c_mk[l, b].rearrange("(k q) d -> q k d", q=128), writes=[K32], dsem=4)
                P.copy("dve", Kb, K32)
                P.dma("sp", K32.ap, c_mv[l, b].rearrange("(k q) d -> q k d", q=128), writes=[K32], dsem=4)
                P.copy("dve", memV[l], K32)
                for kb in range(2):
                    for c in range(8):
                        P.transpose(psb[:, c * 128:(c + 1) * 128], Kb[:, kb, c * 128:(c + 1) * 128], identB, inc=(c == 7))
                    for c in range(8):
                        P.copy("act" if c % 2 == 0 else "dve", memK[l][c][:, kb * 128:(kb + 1) * 128], psb[:, c * 128:(c + 1) * 128])
                cross_core(memK[l], memV[l], b, 1, b)

        import os as _os
        tiles = [("p", ti) for ti in range(SEQ // TP)] + [("s", 0)]
        if debug_tiles is not None:
            tiles = debug_tiles
        layers = list(range(L)) if debug_layers is None else debug_layers
        for kind, ti in tiles:
            sample = kind == "s"
            T = NS_ if sample else TP
            for c in range(8):
                if sample:
                    P.dma("sp", xT[c].ap[:, 0:T], xs_in[c * 128:(c + 1) * 128, :], writes=[xT[c]], dsem=2)
                else:
                    P.dma("sp", xT[c].ap, xp[c * 128:(c + 1) * 128, ti * TP:(ti + 1) * TP], writes=[xT[c]], dsem=2)
            try:
                for l in layers:
                    layer(l, T, sample, ti)
            except _Stop:
                pass
            for c in range(8):
                if sample:
                    _src = {'macc': macc, 'hb': hb, 'a8': a8}.get(_os.environ.get('DUMP', ''), xT)[c]
                    if _src.ap.dtype != F32:
                        P.copy('dve', tmpF[0][:, 0:TP], _src)
                        _src = tmpF[0][:, 0:TP]
                    P.dma("sp", ys[c * 128:(c + 1) * 128, :], _src.ap[:, 0:T], reads=[_src], dsem=3)
                else:
                    _src = {'macc': macc, 'hb': hb, 'a8': a8}.get(_os.environ.get('DUMP', ''), xT)[c]
                    if _src.ap.dtype != F32:
                        P.copy('dve', tmpF[0][:, 0:TP], _src)
                        _src = tmpF[0][:, 0:TP]
                    P.dma("sp", yp[c * 128:(c + 1) * 128, ti * TP:(ti + 1) * TP], _src.ap, reads=[_src], dsem=3)
        P.emit(sems, block)
        import os as _os
        if _os.environ.get('DUMPTRACE'):
            for t in P.trace[-int(_os.environ['DUMPTRACE']):]:
                print('TR', t)
        print("ops per engine", P.n_ops, "final counts", P.final_counts, "arena", P.arena_off)
    return nc


def _consts():
    import ml_dtypes
    cst = np.zeros((128, NCS), np.float32)
    cst[:, 0:128] = np.eye(128, dtype=np.float32)
    s = np.arange(128)[:, None]; l_ = np.arange(128)[None, :]
    cst[:, 128:256] = (s <= l_).astype(np.float32)
    cst[:, 256:384] = 1.0
    own = np.where(s <= l_, 0.0, NEG).astype(np.float32)
    prev = np.where(s > l_, 0.0, NEG).astype(np.float32)
    cst[:, 384:896] = np.tile(own, (1, 4))
    cst[:, 896:1408] = np.tile(prev, (1, 4))
    cb = np.zeros((128, NCB), np.float32)
    cb[:, 0:128] = np.eye(128)
    cb[:, 128:256] = 1.0 / 1024
    cb[:, 256:384] = 1.0 / 2048
    cb[:, 384:512] = 1.0
    cb[:, 512:1024] = np.tile(np.where(l_ >= s, 0.0, NEG), (1, 4))
    cb[:, 1024:1152] = (s <= l_).astype(np.float32)
    cstb = cb.astype(ml_dtypes.bfloat16)

    def rope(pos):
        inv = (500000.0 ** (-np.arange(8, dtype=np.float32) / 8)).astype(np.float32)
        ang = pos.astype(np.float32)[:, None] * inv[None, :]
        cos = np.cos(ang).astype(np.float32); sin = np.sin(ang).astype(np.float32)
        out = np.zeros((128, 2, len(pos)), np.float32)
        for p in range(128):
            d = p % 64
            if d < 8:
                out[p, 0] = cos[:, d]; out[p, 1] = -sin[:, d]
            elif d < 16:
                out[p, 0] = cos[:, d - 8]; out[p, 1] = sin[:, d - 8]
            else:
                out[p, 0] = 1.0
        return out
    ropeP = rope(np.arange(SEQ)); ropeS = rope(np.full((NS_,), 8192))
    return cst, cstb, ropeP, ropeS


def _colmajor(v, n):
    return np.ascontiguousarray(v.reshape(n, 128).T)


def kernel(**inp):
    return _run(inp)


_PREP = {}


def _run(inp, debug_tiles=None, debug_layers=None, cores=None, stop=None):
    f32 = np.float32
    cst, cstb, ropeP, ropeS = _consts()
    w_in = inp["w_in"]
    wqk = np.zeros((L, D, 2560), f32)
    wao = np.zeros((L, D, D), f32)
    for c in range(8):
        for half in range(2):
            h = myhead(c, half)
            cols = C_Q + h * 64 + np.arange(64)
            part = cols.copy()
            part[0:8] = cols[8:16]; part[8:16] = cols[0:8]
            wqk[:, :, c * 256 + half * 64: c * 256 + half * 64 + 64] = w_in[:, :, cols]
            wqk[:, :, c * 256 + 128 + half * 64: c * 256 + 128 + half * 64 + 64] = w_in[:, :, part]
            wao[:, c * 128 + half * 64: c * 128 + half * 64 + 64, :] = inp["w_attn_out"][:, h * 64:(h + 1) * 64, :]
    for kc in range(2):
        for half in range(2):
            g = 2 * kc + half
            cols = C_K + g * 64 + np.arange(64)
            part = cols.copy()
            part[0:8] = cols[8:16]; part[8:16] = cols[0:8]
            o = 2048 + kc * 256
            wqk[:, :, o + half * 64:o + half * 64 + 64] = w_in[:, :, cols]
            wqk[:, :, o + 128 + half * 64:o + 128 + half * 64 + 64] = w_in[:, :, part]
    pcol = np.zeros((L, 128, NPC), f32)
    for l in range(L):
        def put(name, arr):
            pcol[l, :, PC[name]:PC[name] + arr.shape[1]] = arr
        put("g_mix_pre", _colmajor(inp["norm_mix_pre"][l], 8)); put("g_mix_post", _colmajor(inp["norm_mix_post"][l], 8))
        put("g_x_pre", _colmajor(inp["norm_x_pre"][l], 8)); put("g_x_post", _colmajor(inp["norm_x_post"][l], 8))
        put("g_ffn_pre", _colmajor(inp["norm_ffn_pre"][l], 8)); put("g_ffn_post", _colmajor(inp["norm_ffn_post"][l], 8))
        put("g_mem", _colmajor(inp["norm_mem"][l], 8))
        put("conv_a_w", np.concatenate([_colmajor(inp["conv_a_w"][l, j], 8) for j in range(3)], 1))
        put("ssm_conv_w", np.concatenate([_colmajor(inp["ssm_conv_w"][l, j], 32) for j in range(4)], 1))
        put("ssm_conv_b", _colmajor(inp["ssm_conv_b"][l], 32))
        put("ssm_norm", _colmajor(inp["ssm_norm"][l], 16))
        put("ssm_d", _colmajor(np.repeat(inp["ssm_d"][l], 64), 16))
        put("ffn_conv_w", np.concatenate([_colmajor(inp["ffn_conv_w"][l, j], NKF) for j in range(3)], 1))
        put("ffn_conv_b", _colmajor(inp["ffn_conv_b"][l], NKF))
        col = np.zeros((128, 1), f32); col[0:32, 0] = inp["ssm_dt_bias"][l]; put("dtb", col)
        col = np.zeros((128, 1), f32); col[0:32, 0] = inp["ssm_a_log"][l]; put("alog", col)
        put("sinks_bc", np.broadcast_to(inp["attn_sinks"][l][None, :], (128, 16)))
        put("alog_bc", np.broadcast_to(inp["ssm_a_log"][l][None, :], (128, 32)))
        put("dtb_bc", np.broadcast_to(inp["ssm_dt_bias"][l][None, :], (128, 32)))
    shared = {"w_in": w_in, "wqk": wqk, "w_a_out": inp["w_a_out"], "w_ssm_out": inp["w_ssm_out"], "wao": wao,
              "w_out": inp["w_out"], "w_xq": inp["w_xq"], "w_xk": inp["w_xk"], "w_xv": inp["w_xv"], "w_xo": inp["w_xo"],
              "w_ffn_in": inp["w_ffn_in"], "w_ffn_out": inp["w_ffn_out"], "pcol": pcol, "cst": cst, "cstb": cstb,
              "ropeP": ropeP, "ropeS": ropeS}
    shared = {k: np.ascontiguousarray(v) for k, v in shared.items()}

    def st_lay(a, nch, W_):
        x = a.reshape(L, NS_, W_, nch, 128)
        return np.ascontiguousarray(x.transpose(0, 4, 3, 2, 1).reshape(L, 128, nch, W_ * NS_))
    in_maps = []
    for c in range(8):
        bs = slice(c * NS_, (c + 1) * NS_)
        m = dict(shared)
        m["xp"] = np.ascontiguousarray(inp["x_prompt"][c].T)
        m["xs"] = np.ascontiguousarray(inp["x_sample"][bs, 0, :].T)
        m["memT"] = np.ascontiguousarray(inp["mem_prompt"][c].T)
        m["st_ca"] = st_lay(inp["state_conv_a"][:, bs], 8, 2)
        m["st_sc"] = st_lay(inp["state_ssm_conv"][:, bs], 32, 3)
        m["st_ff"] = st_lay(inp["state_ffn_conv"][:, bs], NKF, 2)
        m["st_ssm"] = np.ascontiguousarray(inp["state_ssm"][:, bs].reshape(L, NS_, 2048, 128))
        m["c_sk"] = np.ascontiguousarray(inp["cache_swa_k"][:, bs].reshape(L, NS_, 128, 256))
        m["c_sv"] = np.ascontiguousarray(inp["cache_swa_v"][:, bs].reshape(L, NS_, 128, 256))
        m["c_mk"] = np.ascontiguousarray(inp["cache_mem_k"][:, bs].reshape(L, NS_, 256, 1024))
        m["c_mv"] = np.ascontiguousarray(inp["cache_mem_v"][:, bs].reshape(L, NS_, 256, 1024))
        in_maps.append(m)
    nc = build_program(debug_tiles, debug_layers, stop)
    if cores is not None:
        res = run_bass_kernel_spmd(nc, [in_maps[c] for c in cores], core_ids=list(range(len(cores))))
        R = [res.results[0]] * 8
        for i, c in enumerate(cores):
            R[c] = res.results[i]
        return assemble(R)
    res = run_bass_kernel_spmd(nc, in_maps, core_ids=list(range(8)))
    R = res.results
    return assemble(R)


def assemble(R):
    f32 = np.float32
    yp = np.stack([R[c]["yp"].T for c in range(8)])
    ys = np.concatenate([R[c]["ys"].T[:, None, :] for c in range(8)], 0)

    def un_p(key, nch, W_):
        return np.stack([R[c][key].transpose(0, 3, 2, 1).reshape(L, W_, nch * 128) for c in range(8)], 1)

    def un_s(key, nch, W_):
        outs = []
        for c in range(8):
            a = R[c][key].reshape(L, 128, nch, W_, NS_).transpose(0, 4, 3, 2, 1).reshape(L, NS_, W_, nch * 128)
            outs.append(a)
        return np.concatenate(outs, 1)
    p_conv_a = un_p("o_pca", 8, 2); p_ssm_conv = un_p("o_psc", 32, 3); p_ffn = un_p("o_pffn", NKF, 2)
    p_ssm = np.stack([R[c]["o_pssm"].reshape(L, 128, 32, 64).transpose(0, 2, 3, 1) for c in range(8)], 1)
    p_k = np.stack([R[c]["o_pk"].transpose(0, 3, 1, 2).reshape(L, 128, 4, 64) for c in range(8)], 1)
    p_v = np.stack([R[c]["o_pv"].reshape(L, 128, 4, 64) for c in range(8)], 1)
    p_mk = np.stack([R[c]["o_pmk"].transpose(0, 3, 1, 2).reshape(L, 256, 4, 256) for c in range(8)], 1)
    p_mv = np.stack([R[c]["o_pmv"].reshape(L, 256, 4, 256) for c in range(8)], 1)
    s_conv_a = un_s("o_sca", 8, 2); s_ssm_conv = un_s("o_ssc", 32, 3); s_ffn = un_s("o_sffn", NKF, 2)
    s_ssm = np.concatenate([R[c]["o_sssm"].reshape(L, NS_, 32, 64, 128) for c in range(8)], 1)
    s_k = np.concatenate([R[c]["o_sk"].reshape(L, NS_, 128, 4, 64) for c in range(8)], 1)
    s_v = np.concatenate([R[c]["o_sv"].reshape(L, NS_, 128, 4, 64) for c in range(8)], 1)
    outs = (yp, ys, p_conv_a, p_ssm_conv, p_ssm, p_k, p_v, p_ffn, p_mk, p_mv,
            s_conv_a, s_ssm_conv, s_ssm, s_k, s_v, s_ffn)
    return tuple(np.ascontiguousarray(o, dtype=f32) for o in outs)
```

```python
import numpy as np
import concourse.bass as bass
import concourse.mybir as mybir
from concourse.bass_utils import run_bass_kernel_spmd

F32 = mybir.dt.float32
BF16 = mybir.dt.bfloat16
AF = mybir.ActivationFunctionType
ALU = mybir.AluOpType
AX = mybir.AxisListType

DEFAULT_SERIALK = 1
import os as _osg
FREE_OK = not bool(_osg.environ.get('NOFREE'))
GRAN = 128


class V:
    def __init__(self, ap, keys):
        self.ap = ap
        self.keys = tuple(keys)

    def __getitem__(self, idx):
        return V(self.ap[idx], self.keys)


class Op:
    __slots__ = ("eng", "fn", "reads", "writes", "inc", "dma", "dsem", "ser")

    def __init__(self, eng, fn, reads, writes, inc=True, dma=False, dsem=None):
        self.eng = eng
        self.fn = fn
        self.reads = reads
        self.writes = writes
        self.inc = inc
        self.dma = dma
        self.dsem = dsem
        self.ser = True


def _keys(lst):
    out = []
    for x in lst:
        if x is None:
            continue
        if isinstance(x, V):
            out.extend(x.keys)
        elif isinstance(x, (list, tuple)) and len(x) > 0 and isinstance(x[0], V):
            for y in x:
                out.extend(y.keys)
        else:
            out.append(x)
    return out


class Prog:
    ENGS = ("pe", "act", "dve", "pool", "sp")

    def __init__(self, nc, arena_bytes, n_dsem=6):
        self.nc = nc
        self.ops = []
        self.cur = self.ops
        self.arena_bytes = arena_bytes
        self.arena_off = 0
        self.n_dsem = n_dsem
        self.arena = None
        self.psum_banks = []
        self.dram_cnt = 0

    def alloc(self, shape, dtype, name=None):
        esz = 4 if dtype == F32 else 2
        n = int(np.prod(shape))
        nbytes = n * esz
        nb = (nbytes + GRAN - 1) // GRAN * GRAN
        off = self.arena_off
        self.arena_off += nb
        assert self.arena_off <= self.arena_bytes, ("SBUF arena overflow", self.arena_off)
        ap = self.arena[:, off // 2:(off + nbytes) // 2]
        if dtype == F32:
            ap = ap.bitcast(F32)
        if len(shape) == 2:
            ap = ap.rearrange("p (a b) -> p a b", a=shape[0])
        elif len(shape) == 3:
            ap = ap.rearrange("p (a b c) -> p a b c", a=shape[0], b=shape[1])
        keys = [("sb", g) for g in range(off // GRAN, (off + nb) // GRAN)]
        return V(ap, keys)

    def allocn(self, n, shape, dtype):
        return [self.alloc(shape, dtype) for _ in range(n)]

    def op(self, eng, fn, reads=(), writes=(), inc=True):
        o_ = Op(eng, fn, _keys(reads), _keys(writes), inc=inc)
        o_.ser = getattr(self, 'ser', True)
        self.cur.append(o_)

    def dma(self, q, out_ap, in_ap, reads=(), writes=(), dsem=0):
        def fn(e, out_ap=out_ap, in_ap=in_ap):
            return e.dma_start(out=out_ap, in_=in_ap)
        self.cur.append(Op(q, fn, _keys(reads), _keys(writes), dma=True, dsem=dsem))

    def slot(self):
        s = []
        self.cur.append(s)
        return s

    def mm(self, out, lhsT, rhs, start, stop, inc=None, **kw):
        if inc is None:
            inc = stop
        def fn(e, o=out.ap, l=lhsT.ap, r=rhs.ap):
            return e.matmul(o, lhsT=l, rhs=r, start=start, stop=stop, **kw)
        self.op("pe", fn, reads=[lhsT, rhs], writes=[out], inc=inc)

    def transpose(self, out, in_, ident, inc=True):
        def fn(e, o=out.ap, i=in_.ap, d=ident.ap):
            return e.transpose(o, i, d)
        self.op("pe", fn, reads=[in_, ident], writes=[out], inc=inc)

    def act(self, out, in_, func, bias=None, scale=1.0, accum=None, extra_reads=()):
        def fn(e, o=out.ap, i=in_.ap):
            kw = {}
            if bias is not None:
                kw["bias"] = bias.ap if isinstance(bias, V) else bias
            if accum is not None:
                kw["accum_out"] = accum.ap
            sc = scale.ap if isinstance(scale, V) else scale
            return e.activation(out=o, in_=i, func=func, scale=sc, **kw)
        rd = [in_] + [b for b in (bias, scale) if isinstance(b, V)] + list(extra_reads)
        wr = [out] + ([accum] if accum is not None else [])
        self.op("act", fn, reads=rd, writes=wr)

    def tt(self, eng, out, in0, in1, op):
        def fn(e, o=out.ap, a=in0.ap, b=in1.ap):
            return e.tensor_tensor(out=o, in0=a, in1=b, op=op)
        self.op(eng, fn, reads=[in0, in1], writes=[out])

    def ts(self, eng, out, in0, s1, op0, s2=None, op1=None):
        def fn(e, o=out.ap, a=in0.ap):
            a1 = s1.ap if isinstance(s1, V) else s1
            a2 = s2.ap if isinstance(s2, V) else s2
            if op1 is None:
                return e.tensor_scalar(out=o, in0=a, scalar1=a1, scalar2=None, op0=op0)
            return e.tensor_scalar(out=o, in0=a, scalar1=a1, scalar2=a2, op0=op0, op1=op1)
        rd = [in0] + [s for s in (s1, s2) if isinstance(s, V)]
        self.op(eng, fn, reads=rd, writes=[out])

    def stt(self, eng, out, in0, scalar, in1, op0, op1):
        def fn(e, o=out.ap, a=in0.ap, b=in1.ap):
            s = scalar.ap if isinstance(scalar, V) else scalar
            return e.scalar_tensor_tensor(out=o, in0=a, scalar=s, in1=b, op0=op0, op1=op1)
        rd = [in0, in1] + ([scalar] if isinstance(scalar, V) else [])
        self.op(eng, fn, reads=rd, writes=[out])

    def copy(self, eng, out, in_):
        if eng == "act":
            def fn(e, o=out.ap, i=in_.ap):
                return e.activation(out=o, in_=i, func=AF.Copy)
        else:
            def fn(e, o=out.ap, i=in_.ap):
                return e.tensor_copy(out=o, in_=i)
        self.op(eng, fn, reads=[in_], writes=[out])

    def memset(self, eng, out, val):
        def fn(e, o=out.ap):
            return e.memset(o, val)
        self.op(eng, fn, writes=[out])

    def recip(self, out, in_):
        def fn(e, o=out.ap, i=in_.ap):
            return e.reciprocal(out=o, in_=i)
        self.op("dve", fn, reads=[in_], writes=[out])

    def reduce(self, out, in_, op, axis=None):
        def fn(e, o=out.ap, i=in_.ap):
            return e.tensor_reduce(out=o, in_=i, axis=axis or AX.X, op=op)
        self.op("dve", fn, reads=[in_], writes=[out])

    def _flatten(self, lst, out):
        for x in lst:
            if isinstance(x, list):
                self._flatten(x, out)
            else:
                out.append(x)

    def _analyze(self, flat, sems):
        cnt = {k: 0 for k in sems}
        last_w = {}
        readers = {}
        waited = {e: {} for e in self.ENGS}
        streams = {e: [] for e in self.ENGS}
        pending = []
        trace = []
        import os as _os2
        SERIAL = not bool(_os2.environ.get('NOSERIAL'))
        SERIALK = int(_os2.environ.get('SERIALK', DEFAULT_SERIALK))
        chain = []
        prev_ser = True
        for op in flat:
            deps = {}
            for k in op.reads:
                lw = last_w.get(k)
                if lw is not None:
                    if deps.get(lw[0], 0) < lw[1]:
                        deps[lw[0]] = lw[1]
            for k in op.writes:
                lw = last_w.get(k)
                if lw is not None:
                    if deps.get(lw[0], 0) < lw[1]:
                        deps[lw[0]] = lw[1]
                rs = readers.get(k)
                if rs:
                    for s, v in rs.items():
                        if deps.get(s, 0) < v:
                            deps[s] = v
            if SERIAL and not op.dma and op.ser and len(chain) >= SERIALK:
                pt = chain[-SERIALK]
                if pt[1] <= cnt[pt[0]] and deps.get(pt[0], 0) < pt[1]:
                    deps[pt[0]] = pt[1]
                if not prev_ser:
                    for e_ in ("pe", "act", "dve"):
                        if cnt[e_] > 0 and deps.get(e_, 0) < cnt[e_]:
                            deps[e_] = cnt[e_]
            if not op.dma:
                prev_ser = op.ser
            for s_ in list(deps.keys()):
                if s_[0] == 'd' and s_ != 'dve':
                    deps[s_] = cnt[s_]
            waits = []
            wd = waited[op.eng]
            for s, v in deps.items():
                if s == op.eng and (s == "pe" or v > cnt[s]):
                    continue
                if s == "pe" and v > cnt["pe"]:
                    for p in pending:
                        p.inc = True
                    return None
                if wd.get(s, 0) < v:
                    wd[s] = v
                    waits.append((s, v))
            if op.dma:
                sname = "d%d" % op.dsem
                cnt[sname] += 16
                tok = (sname, cnt[sname])
                incinfo = (sname, 16)
            else:
                sname = op.eng
                if op.inc:
                    cnt[sname] += 1
                    tok = (sname, cnt[sname])
                    incinfo = (sname, 1)
                    if sname == "pe":
                        pending = []
                else:
                    tok = (sname, cnt[sname] + 1)
                    incinfo = None
                    if sname == "pe":
                        pending.append(op)
            for k in op.reads:
                readers.setdefault(k, {})[tok[0]] = max(readers.get(k, {}).get(tok[0], 0), tok[1])
            for k in op.writes:
                last_w[k] = tok
                readers[k] = {}
            if not op.dma:
                chain.append(tok)
            streams[op.eng].append((waits, op.fn, incinfo))
            trace.append((op.eng, waits, incinfo, tok, op.fn.__qualname__.split('.')[1] if '.' in op.fn.__qualname__ else op.fn.__qualname__))
        self.trace = trace
        return cnt, streams

    def emit(self, sems, block, final_waits_engine="sp"):
        flat = []
        self._flatten(self.ops, flat)
        while True:
            res = self._analyze(flat, sems)
            if res is not None:
                break
        cnt, streams = res
        self.final_counts = dict(cnt)
        self.n_ops = {e: len(streams[e]) for e in self.ENGS}

        def run(e, lst, is_last):
            for waits, fn, incinfo in lst:
                for s, v in waits:
                    e.wait_ge(sems[s], v)
                ins = fn(e)
                if incinfo is not None:
                    ins.then_inc(sems[incinfo[0]], incinfo[1])
            if is_last:
                for s, v in cnt.items():
                    if v > 0:
                        e.wait_ge(sems[s], v)

        @block.tensor
        def _(e):
            run(e, streams["pe"], False)

        @block.scalar
        def _(e):
            run(e, streams["act"], False)

        @block.vector
        def _(e):
            run(e, streams["dve"], False)

        @block.gpsimd
        def _(e):
            run(e, streams["pool"], False)

        @block.sync
        def _(e):
            run(e, streams["sp"], True)

L = 2
D = 1024
SEQ = 2048
TP = 256
NS_ = 16
DFF = 2816
NKF = 22
C_V, C_GB, C_GC, C_Z, C_XS, C_B, C_C, C_DT, C_Q, C_K, C_VV, C_GA, C_GBB, C_GCC = (
    0, 1024, 2048, 3072, 5120, 7168, 8192, 9216, 9248, 10272, 10528, 10784, 11808, 12832)
NEG = -30000.0
EPS = 1e-6

PC = {}
_o = 0
for _n, _w in [("g_mix_pre", 8), ("g_mix_post", 8), ("g_x_pre", 8), ("g_x_post", 8), ("g_ffn_pre", 8),
               ("g_ffn_post", 8), ("g_mem", 8), ("conv_a_w", 24), ("ssm_conv_w", 128), ("ssm_conv_b", 32),
               ("ssm_norm", 16), ("ssm_d", 16), ("ffn_conv_w", 66), ("ffn_conv_b", 22), ("dtb", 1), ("alog", 1),
               ("sinks_bc", 16), ("alog_bc", 32), ("dtb_bc", 32)]:
    PC[_n] = _o
    _o += _w
NPC = _o
PC_N = {"conv_a_w": 8, "ssm_conv_w": 32, "ffn_conv_w": 22}
CS = {"ident": 0, "U": 128, "ones": 256, "maskOwn": 384, "maskPrev": 896, "eexp": 1408}
NCS = 1408
CB = {"ident": 0, "onesD": 128, "ones2D": 256, "ones": 384, "negmask4": 512}
NCB = 1152


def myhead(c, half):
    return (c // 4) * 8 + half * 4 + (c % 4)


class _Stop(Exception):
    pass


def build_program(debug_tiles=None, debug_layers=None, stop=None):
    from contextlib import ExitStack
    nc = bass.Bass("TRN2", target_bir_lowering=False)

    def din(name, shape, dt=F32):
        return nc.dram_tensor(name, list(shape), dt, kind="ExternalInput").ap()

    def dout(name, shape, dt=F32):
        return nc.dram_tensor(name, list(shape), dt, kind="ExternalOutput").ap()

    xp = din("xp", [D, SEQ]); xs_in = din("xs", [D, NS_]); memT = din("memT", [D, 256])
    W = {}
    for n, shp in [("w_in", [L, D, 13856]), ("wqk", [L, D, 2560]), ("w_a_out", [L, D, D]),
                   ("w_ssm_out", [L, 2048, D]), ("wao", [L, D, D]), ("w_out", [L, D, D]),
                   ("w_xq", [L, D, D]), ("w_xk", [L, D, D]), ("w_xv", [L, D, D]), ("w_xo", [L, D, D]),
                   ("w_ffn_in", [L, D, 2 * DFF]), ("w_ffn_out", [L, DFF, D])]:
        W[n] = din(n, shp)
    pcol_d = din("pcol", [L, 128, NPC]); cst_d = din("cst", [128, NCS]); cstb_d = din("cstb", [128, NCB], BF16)
    ropeP_d = din("ropeP", [128, 2, SEQ]); ropeS_d = din("ropeS", [128, 2, NS_])
    st_ca = din("st_ca", [L, 128, 8, 2 * NS_]); st_sc = din("st_sc", [L, 128, 32, 3 * NS_])
    st_ff = din("st_ff", [L, 128, NKF, 2 * NS_]); st_ssm = din("st_ssm", [L, NS_, 2048, 128])
    c_sk = din("c_sk", [L, NS_, 128, 256]); c_sv = din("c_sv", [L, NS_, 128, 256])
    c_mk = din("c_mk", [L, NS_, 256, 1024]); c_mv = din("c_mv", [L, NS_, 256, 1024])

    yp = dout("yp", [D, SEQ]); ys = dout("ys", [D, NS_])
    o_pca = dout("o_pca", [L, 128, 8, 2]); o_psc = dout("o_psc", [L, 128, 32, 3]); o_pssm = dout("o_pssm", [L, 128, 2048])
    o_pk = dout("o_pk", [L, 2, 128, 128]); o_pv = dout("o_pv", [L, 128, 256]); o_pffn = dout("o_pffn", [L, 128, NKF, 2])
    o_pmk = dout("o_pmk", [L, 8, 128, 256]); o_pmv = dout("o_pmv", [L, 256, 1024])
    o_sca = dout("o_sca", [L, 128, 8, 2 * NS_]); o_ssc = dout("o_ssc", [L, 128, 32, 3 * NS_])
    o_sssm = dout("o_sssm", [L, NS_, 2048, 128]); o_sk = dout("o_sk", [L, NS_, 128, 256]); o_sv = dout("o_sv", [L, NS_, 128, 256])
    o_sffn = dout("o_sffn", [L, 128, NKF, 2 * NS_])

    ARENA = 190 * 1024
    P = Prog(nc, ARENA)
    with ExitStack() as es:
        P.arena = es.enter_context(nc.sbuf_tensor("arena", [128, ARENA // 2], BF16))
        ps = [V(es.enter_context(nc.psum_tensor("ps%d" % i, [128, 512], F32))[:, :], [("ps", i)]) for i in range(7)]
        psb = V(es.enter_context(nc.psum_tensor("psb", [128, 1024], BF16))[:, :], [("ps", 7)])
        semn = ["pe", "act", "dve", "pool"] + ["d%d" % i for i in range(6)]
        sems = {k: es.enter_context(nc.semaphore(k)) for k in semn}
        block = es.enter_context(nc.Block())

        cst = P.alloc([NCS], F32); cstb = P.alloc([NCB], BF16)
        pcol = [P.alloc([NPC], F32) for _ in range(L)]
        P.dma("sp", cst.ap, cst_d, writes=[cst], dsem=1)
        P.dma("sp", cstb.ap, cstb_d, writes=[cstb], dsem=1)
        for l in range(L):
            P.dma("sp", pcol[l].ap, pcol_d[l], writes=[pcol[l]], dsem=1)
        identF = cst[:, 0:128]; UF = cst[:, 128:256]; onesF = cst[:, 256:384]
        maskOwn = cst[:, 384:896]; maskPrev = cst[:, 896:1408]
        identB = cstb[:, 0:128]; onesD = cstb[:, 128:256]; ones2D = cstb[:, 256:384]; onesB = cstb[:, 384:512]
        negmask4 = cstb[:, 512:1024]
        UB = cstb[:, 1024:1152]
        epsc = P.alloc([1], F32)
        P.memset("dve", epsc, EPS)

        def pc(l, name, i, n=1):
            o = PC[name] + i
            return pcol[l][:, o:o + n]

        NSTG, NRING, PF = 1, 2, 1
        stage = [P.alloc([8, 512], F32) for _ in range(NSTG)]
        ring = [P.alloc([8, 512], BF16) for _ in range(NRING)]
        ws = {"i": 0, "slots": [], "init": P.slot()}

        def wslot(k):
            return ws["init"] if k < 0 else ws["slots"][k]

        def wget(name, l, row0, nkc, segs):
            i = ws["i"]; ws["i"] += 1
            st = stage[i % NSTG]; rg = ring[i % NRING]
            src = W[name][l]
            old = P.cur
            P.cur = wslot(i - PF - NSTG)
            off = 0
            for (c0, n) in segs:
                P.dma("sp", st.ap[:, 0:nkc, off:off + n],
                      src[row0:row0 + nkc * 128, c0:c0 + n].rearrange("(kc p) n -> p kc n", p=128),
                      writes=[st], dsem=0)
                off += n
            P.cur = wslot(i - PF)
            P.copy("act" if (i % 2 == 0) else "dve", rg[:, 0:nkc, 0:off], st[:, 0:nkc, 0:off])
            P.cur = old
            ws["slots"].append(P.slot())
            return rg

        bankc = {"i": 0}

        def bank():
            b = ps[bankc["i"] % 6]
            bankc["i"] += 1
            return b

        xT = [P.alloc([TP], F32) for _ in range(8)]
        hb = [P.alloc([TP], BF16) for _ in range(8)]
        macc = [P.alloc([TP], F32) for _ in range(8)]
        a8 = [P.alloc([TP], BF16) for _ in range(8)]
        _o0 = P.arena_off
        ub = [P.alloc([TP], BF16) for _ in range(NKF)]
        ub_all = V(P.arena[:, _o0 // 2:_o0 // 2 + 16 * TP].rearrange("p (j t) -> p j t", j=16), [k for u in ub[0:16] for k in u.keys])
        sq = [P.alloc([TP], BF16) for _ in range(2)]
        rstd = P.alloc([TP], F32); rt = P.alloc([TP], F32)
        tmpF = [P.alloc([512], F32) for _ in range(3)]
        gsb = [P.alloc([TP], BF16) for _ in range(2)]
        ext = [P.alloc([48 + TP], F32) for _ in range(2)]
        haloA = [P.alloc([8, 2], F32) for _ in range(L)]
        haloS = [P.alloc([32, 3], F32) for _ in range(L)]
        haloF = [P.alloc([NKF, 2], F32) for _ in range(L)]
        for l in range(L):
            P.memset("dve", haloA[l], 0.0); P.memset("dve", haloS[l], 0.0); P.memset("dve", haloF[l], 0.0)
        _o1 = P.arena_off
        xsf = [P.alloc([TP], BF16) for _ in range(16)]
        xsf_all = V(P.arena[:, _o1 // 2:_o1 // 2 + 16 * TP].rearrange("p (j t) -> p j t", j=16), [k for u in xsf for k in u.keys])
        Bf = [P.alloc([TP], BF16) for _ in range(8)]
        Cf = [P.alloc([TP], BF16) for _ in range(8)]
        hT = [P.alloc([2048], F32) for _ in range(L)]
        hTb = P.alloc([2048], BF16)
        for l in range(L):
            P.memset("dve", hT[l], 0.0)
        xdt_tm = P.alloc([2048], BF16); xw_tm = P.alloc([2048], BF16); B_tm = P.alloc([1024], BF16)
        a_hi = P.alloc([32], BF16); a_lo = P.alloc([32], BF16); a_res = P.alloc([32], F32)
        dt_tm = P.alloc([32], F32); a_tm = P.alloc([32], F32); cs_tm = P.alloc([32], F32)
        dcs_tm = P.alloc([32], F32); ecl_bc = P.alloc([32], F32); A_bc = [P.alloc([32], F32) for _ in range(L)]
        ecsb = P.alloc([4, 128], BF16); CH = [P.alloc([128], BF16) for _ in range(2)]
        argb = P.alloc([4, 128], F32); Eb = argb; MT = [P.alloc([4, 128], BF16) for _ in range(2)]
        qr = [P.alloc([TP], BF16) for _ in range(8)]
        kf = [[P.alloc([128 + TP], BF16) for _ in range(2)] for _ in range(L)]
        vtm = [P.alloc([TP // 128 + 1, 256], BF16) for _ in range(L)]
        kf32 = [P.alloc([128], F32) for _ in range(2)]
        v32 = P.alloc([256], F32)
        smb = P.alloc([512], F32); Ebf = [P.alloc([512], BF16) for _ in range(2)]
        esink = [P.alloc([16], F32) for _ in range(L)]
        memK = [[P.alloc([256], BF16) for _ in range(8)] for _ in range(L)]
        memV = [P.alloc([2, 1024], BF16) for _ in range(L)]
        AcolS = [P.alloc([1], F32) for _ in range(L)]

        for l in range(L):
            P.act(esink[l], pc(l, "sinks_bc", 0, 16), AF.Exp)
            P.act(A_bc[l], pc(l, "alog_bc", 0, 32), AF.Exp)
            P.ts("dve", A_bc[l], A_bc[l], -1.0, ALU.mult)
            P.act(AcolS[l], pc(l, "alog", 0, 1), AF.Exp)
            P.ts("dve", AcolS[l], AcolS[l], -1.0, ALU.mult)

        def rms_stats(src_chunks, T, ones_m):
            n = len(src_chunks)
            for c in range(n):
                s = sq[c % 2]
                P.act(s[:, 0:T], src_chunks[c][:, 0:T], AF.Square)
                P.mm(ps[6][:, 0:T], ones_m, s[:, 0:T], c == 0, c == n - 1)
            P.act(rt[:, 0:T], ps[6][:, 0:T], AF.Sqrt, bias=epsc)
            P.recip(rstd[:, 0:T], rt[:, 0:T])

        def norm_to_bf(src_chunks, l, gname, dst_chunks, T, ones_m=None):
            rms_stats(src_chunks, T, ones_m if ones_m is not None else onesD)
            for c in range(len(src_chunks)):
                P.stt("dve", dst_chunks[c][:, 0:T], src_chunks[c][:, 0:T], pc(l, gname, c), rstd[:, 0:T],
                      ALU.mult, ALU.mult)

        def post_norm_residual(l, gname, T):
            rms_stats(macc, T, onesD)
            for c in range(8):
                t = tmpF[c % 3]
                P.stt("dve", t[:, 0:T], macc[c][:, 0:T], pc(l, gname, c), rstd[:, 0:T], ALU.mult, ALU.mult)
                P.tt("dve", xT[c][:, 0:T], xT[c][:, 0:T], t[:, 0:T], ALU.add)

        def proj_fm(pso, wb, nkc, col0, M, rhs_chunks, T, start=True, stop=True, kc0=0):
            for kc in range(nkc):
                P.mm(pso, wb[:, kc, col0:col0 + M], rhs_chunks[kc0 + kc][:, 0:T],
                     start and kc == 0, stop and kc == nkc - 1)

        def conv_chunk(l, T, sample, src, wname, ntap, cidx, nch, halo, st_d, so_d, po_last):
            e = ext[0]; r = ext[1]
            stride = NS_ if sample else 1
            H = (ntap - 1) * stride
            if sample:
                P.dma("sp", e.ap[:, 0:H], st_d[l, :, cidx, :], writes=[e], dsem=2)
            else:
                P.copy("dve", e[:, 0:H], halo[l][:, cidx, :])
            return e, r, H, stride

        def chk(k):
            if stop is not None and k >= stop:
                raise _Stop()

        def layer(l, T, sample, ti):
            t0 = 0 if sample else ti * TP
            last = (not sample) and ti == SEQ // TP - 1
            nblk = 1 if sample else T // 128

            def conv(src, wname, bname, ntap, cidx, halo, st_d, so_d, out_fn):
                e = ext[0]; r = ext[1]
                stride = NS_ if sample else 1
                H = (ntap - 1) * stride
                if sample:
                    P.dma("sp", e.ap[:, 0:H], st_d[l, :, cidx, :], writes=[e], dsem=2)
                else:
                    P.copy("dve", e[:, 0:H], halo[l][:, cidx, :])
                src(e[:, H:H + T])
                for j in range(ntap):
                    wcol = pc(l, wname, j * (PC_N[wname]) + cidx)
                    if j == 0:
                        P.ts("dve", r[:, 0:T], e[:, 0:T], wcol, ALU.mult)
                    else:
                        P.stt("dve", r[:, 0:T], e[:, j * stride:j * stride + T], wcol, r[:, 0:T], ALU.mult, ALU.add)
                if sample:
                    P.dma("sp", so_d[l, :, cidx, :], e.ap[:, stride:stride + H], reads=[e], dsem=3)
                else:
                    P.copy("dve", halo[l][:, cidx, :], e[:, T:T + H])
                out_fn(r[:, 0:T])

            P.ser = not FREE_OK
            norm_to_bf(xT, l, "g_mix_pre", hb, T)
            chk(1)

            for c in range(8):
                wb = wget("w_in", l, 0, 8, [(C_V + c * 128, 128), (C_GB + c * 128, 128), (C_GC + c * 128, 128)])
                pv, pgb, pgc = bank(), bank(), bank()
                proj_fm(pv[:, 0:T], wb, 8, 0, 128, hb, T)
                proj_fm(pgb[:, 0:T], wb, 8, 128, 128, hb, T)
                proj_fm(pgc[:, 0:T], wb, 8, 256, 128, hb, T)
                vs = tmpF[0]
                P.copy("act", vs[:, 0:T], pv[:, 0:T])

                def src(dst, pgc=pgc, vs=vs):
                    P.tt("dve", dst, pgc[:, 0:T], vs[:, 0:T], ALU.mult)

                def outf(rv, pgb=pgb, c=c):
                    P.tt("dve", a8[c][:, 0:T], pgb[:, 0:T], rv, ALU.mult)
                conv(src, "conv_a_w", None, 3, c, haloA, st_ca, o_sca, outf)

            def gated_out(wname, gate_col, in_chunks, nk_in, first):
                nkb = (nk_in + 7) // 8
                for half in range(2):
                    pys = [bank() for _ in range(4)]
                    for kb in range(nkb):
                        nk = min(8, nk_in - kb * 8)
                        wb_ = wget(wname, l, kb * 1024, nk, [(half * 512, 512)])
                        for oc in range(4):
                            proj_fm(pys[oc][:, 0:T], wb_, nk, oc * 128, 128, in_chunks, T,
                                    start=(kb == 0), stop=(kb == nkb - 1), kc0=kb * 8)
                    wg = wget("w_in", l, 0, 8, [(gate_col + half * 512, 512)])
                    for oc in range(4):
                        c = half * 4 + oc
                        pg = ps[6] if False else bank()
                        proj_fm(pg[:, 0:T], wg, 8, oc * 128, 128, hb, T)
                        g = gsb[oc % 2]
                        P.act(g[:, 0:T], pg[:, 0:T], AF.Sigmoid)
                        if first:
                            P.tt("dve", macc[c][:, 0:T], pys[oc][:, 0:T], g[:, 0:T], ALU.mult)
                        else:
                            t = tmpF[oc % 3]
                            P.tt("dve", t[:, 0:T], pys[oc][:, 0:T], g[:, 0:T], ALU.mult)
                            P.tt("dve", macc[c][:, 0:T], macc[c][:, 0:T], t[:, 0:T], ALU.add)

            chk(2)
            gated_out("w_a_out", C_GA, a8, 8, True)
            chk(3)

            dsts = xsf + Bf + Cf
            for blk in range(8):
                wb = wget("w_in", l, 0, 8, [(C_XS + blk * 512, 512)])
                for oc in range(4):
                    c = blk * 4 + oc
                    px = bank()
                    proj_fm(px[:, 0:T], wb, 8, oc * 128, 128, hb, T)

                    def src(dst, px=px):
                        P.copy("act", dst, px[:, 0:T])

                    def outf(rv, c=c):
                        P.act(dsts[c][:, 0:T], rv, AF.Silu, bias=pc(l, "ssm_conv_b", c))
                    conv(src, "ssm_conv_w", None, 4, c, haloS, st_sc, o_ssc, outf)
            chk(4)
            wdt = wget("w_in", l, 0, 8, [(C_DT, 32)])
            P.ser = True
            if not sample:
                for tc in range(T // 128):
                    tk = slice(tc * 128, (tc + 1) * 128)
                    pd = bank()
                    for kc in range(8):
                        P.mm(pd[:, 0:32], hb[kc][:, tk], wdt[:, kc, 0:32], kc == 0, kc == 7)
                    P.tt("dve", dt_tm, pd[:, 0:32], pc(l, "dtb_bc", 0, 32), ALU.add)
                    P.act(dt_tm, dt_tm, AF.Exp)
                    P.act(dt_tm, dt_tm, AF.Ln, bias=1.0)
                    P.tt("dve", a_tm, dt_tm, A_bc[l], ALU.mult)
                    P.copy("dve", a_hi, a_tm)
                    P.tt("dve", a_res, a_tm, a_hi, ALU.subtract)
                    P.copy("dve", a_lo, a_res)
                    chk(4.1)
                    pcs = bank()
                    P.mm(pcs[:, 0:32], UF, a_tm, True, True)
                    P.mm(pcs[:, 32:64], onesF, a_tm, True, True)
                    P.copy("dve", cs_tm, pcs[:, 0:32])
                    P.tt("dve", dcs_tm, pcs[:, 32:64], cs_tm, ALU.subtract)
                    P.act(dcs_tm, dcs_tm, AF.Exp)
                    P.act(ecl_bc, pcs[:, 32:64], AF.Exp)
                    chk(4.2)
                    for hf in range(2):
                        for j in range(8):
                            P.transpose(psb[:, j * 128:(j + 1) * 128], xsf[hf * 8 + j][:, tk], identB, inc=(j == 7))
                        xv = V(psb.ap.rearrange("p (h d) -> p h d", h=16), psb.keys)
                        dv = V(dt_tm.ap[:, hf * 16:(hf + 1) * 16].unsqueeze(2).to_broadcast([128, 16, 64]), dt_tm.keys)
                        ov = V(xdt_tm.ap[:, hf * 1024:(hf + 1) * 1024].rearrange("p (h d) -> p h d", h=16), xdt_tm.keys)
                        P.tt("dve", ov, xv, dv, ALU.mult)
                    xv = V(xdt_tm.ap.rearrange("p (h d) -> p h d", h=32), xdt_tm.keys)
                    dv = V(dcs_tm.ap.unsqueeze(2).to_broadcast([128, 32, 64]), dcs_tm.keys)
                    ov = V(xw_tm.ap.rearrange("p (h d) -> p h d", h=32), xw_tm.keys)
                    P.tt("dve", ov, xv, dv, ALU.mult)
                    chk(4.3)
                    for g in range(8):
                        P.transpose(psb[:, g * 128:(g + 1) * 128], Bf[g][:, tk], identB, inc=(g == 7))
                    P.copy("act", B_tm, psb)
                    P.copy("act", hTb, hT[l])
                    chk(4.4)
                    import os as _os
                    for g in [int(x) for x in _os.environ.get('GLIST', '0,1,2,3,4,5,6,7').split(',')]:
                        pcb = bank()
                        P.mm(pcb[:, 0:128], Bf[g][:, tk], Cf[g][:, tk], True, True)
                        pb = bank()
                        for r in range(4):
                            h = 4 * g + r
                            P.mm(pb[:, r * 128:(r + 1) * 128],
                                 V(a_hi.ap[:, h:h + 1].to_broadcast([128, 128]), a_hi.keys), UB,
                                 True, False, inc=False)
                            P.mm(pb[:, r * 128:(r + 1) * 128],
                                 V(a_lo.ap[:, h:h + 1].to_broadcast([128, 128]), a_lo.keys), UB,
                                 False, True, inc=(r == 3))
                        pb3 = V(pb.ap.rearrange("p (r l) -> p r l", r=4), pb.keys)
                        P.act(ecsb, pb3, AF.Exp)
                        cb = V(cs_tm.ap[:, 4 * g:4 * g + 4].unsqueeze(2).to_broadcast([128, 4, 128]), cs_tm.keys)
                        P.tt("dve", argb, pb3, cb, ALU.subtract)
                        P.tt("dve", argb, argb, V(negmask4.ap.rearrange("p (r l) -> p r l", r=4), negmask4.keys), ALU.add)
                        P.act(Eb, argb, AF.Exp)
                        mt = MT[g % 2]
                        cbv = V(pcb.ap[:, 0:128].unsqueeze(1).to_broadcast([128, 4, 128]), pcb.keys)
                        P.tt("dve", mt, Eb, cbv, ALU.mult)
                        chk(4.5)
                        chk(4.7 + g * 0.01 + 0.001)
                        for jj in ([] if _os.environ.get('SKIPY') else range(2)):
                            j = 2 * g + jj
                            py = bank()
                            for hl in range(2):
                                r = 2 * jj + hl
                                h = 4 * g + r
                                ch = CH[hl]
                                if not _os.environ.get('E2TEST'):
                                    P.tt("dve", ch, Cf[g][:, tk], ecsb[:, r, :], ALU.mult)
                                if _os.environ.get('E2TEST'):
                                    P.mm(py[:, hl * 128:(hl + 1) * 128], xdt_tm[:, j * 128:(j + 1) * 128], mt[:, r, :],
                                         True, True, inc=(hl == 1))
                                    continue
                                if _os.environ.get('M128TEST'):
                                    P.mm(py[:, hl * 128:(hl + 1) * 128], xdt_tm[:, j * 128:(j + 1) * 128], mt[:, r, :],
                                         True, False, inc=False)
                                    P.mm(py[:, hl * 128:(hl + 1) * 128], hTb[:, j * 128:(j + 1) * 128], ch,
                                         False, True, inc=(hl == 1))
                                    continue
                                if _os.environ.get('OFFTEST'):
                                    P.mm(py[0:64, hl * 128:(hl + 1) * 128], xdt_tm[:, h * 64:(h + 1) * 64], mt[:, r, :],
                                         True, False, inc=False)
                                    P.mm(py[0:64, hl * 128:(hl + 1) * 128], hTb[:, h * 64:(h + 1) * 64], ch,
                                         False, True, inc=(hl == 1))
                                    continue
                                P.mm(py[hl * 64:(hl + 1) * 64, 0:128], xdt_tm[:, h * 64:(h + 1) * 64], mt[:, r, :],
                                     True, False, inc=False)
                                P.mm(py[hl * 64:(hl + 1) * 64, 0:128], hTb[:, h * 64:(h + 1) * 64], ch,
                                     False, True, inc=(hl == 1))
                            if not _os.environ.get('E1TEST'):
                                P.stt("dve", ub[j][:, tk], xsf[j][:, tk], pc(l, "ssm_d", j), py[:, 0:128], ALU.mult, ALU.add)
                            chk(4.6)
                            chk(4.7 + g * 0.01 + 0.002 + 0.001 * jj)
                        if _os.environ.get('SKIPST'):
                            continue
                        pst = bank()
                        P.mm(pst[:, 0:256], B_tm[:, g * 128:(g + 1) * 128], xw_tm[:, g * 256:(g + 1) * 256], True, True)
                        hv = V(hT[l].ap[:, g * 256:(g + 1) * 256].rearrange("p (r d) -> p r d", r=4), hT[l].keys)
                        ev = V(ecl_bc.ap[:, 4 * g:4 * g + 4].unsqueeze(2).to_broadcast([128, 4, 64]), ecl_bc.keys)
                        P.tt("dve", hv, hv, ev, ALU.mult)
                        P.tt("dve", hT[l][:, g * 256:(g + 1) * 256], hT[l][:, g * 256:(g + 1) * 256], pst[:, 0:256], ALU.add)
                        chk(4.7)
                        chk(4.7 + g * 0.01 + 0.004)
                    chk(4.8)
                if last:
                    P.dma("sp", o_pssm[l], hT[l].ap, reads=[hT[l]], dsem=3)
                    P.dma("sp", o_pca[l], haloA[l].ap, reads=[haloA[l]], dsem=3)
                    P.dma("sp", o_psc[l], haloS[l].ap, reads=[haloS[l]], dsem=3)
            else:
                ssd_sample(l, wdt)
            chk(5)
            P.ser = not FREE_OK
            for blk in range(4):
                wb = wget("w_in", l, 0, 8, [(C_Z + blk * 512, 512)])
                for oc in range(4):
                    j = blk * 4 + oc
                    pz = bank()
                    proj_fm(pz[:, 0:T], wb, 8, oc * 128, 128, hb, T)
                    g = gsb[oc % 2]
                    P.act(g[:, 0:T], pz[:, 0:T], AF.Silu)
                    P.tt("dve", ub[j][:, 0:T], ub[j][:, 0:T], g[:, 0:T], ALU.mult)
            rms_stats(ub[0:16], T, ones2D)
            for j in range(16):
                P.stt("dve", ub[j][:, 0:T], ub[j][:, 0:T], pc(l, "ssm_norm", j), rstd[:, 0:T], ALU.mult, ALU.mult)
            gated_out("w_ssm_out", C_GBB, ub, 16, False)
            chk(6)

            rope_d = ropeS_d if sample else ropeP_d
            cosb = tmpF[1]; sinb = tmpF[2]
            P.dma("sp", cosb.ap[:, 0:T], rope_d[:, 0, t0:t0 + T], writes=[cosb], dsem=2)
            P.dma("sp", sinb.ap[:, 0:T], rope_d[:, 1, t0:t0 + T], writes=[sinb], dsem=2)
            for blk in range(5):
                wb = wget("wqk", l, 0, 8, [(blk * 512, 512)])
                for i2 in range(2):
                    c = blk * 2 + i2
                    pq = bank(); pp = bank()
                    proj_fm(pq[:, 0:T], wb, 8, i2 * 256, 128, hb, T)
                    proj_fm(pp[:, 0:T], wb, 8, i2 * 256 + 128, 128, hb, T)
                    t1 = tmpF[0]
                    P.tt("dve", t1[:, 0:T], pq[:, 0:T], cosb[:, 0:T], ALU.mult)
                    t2 = smb
                    P.tt("dve", t2[:, 0:T], pp[:, 0:T], sinb[:, 0:T], ALU.mult)
                    if c < 8:
                        P.tt("dve", qr[c][:, 0:T], t1[:, 0:T], t2[:, 0:T], ALU.add)
                    else:
                        kc_ = c - 8
                        if sample:
                            P.tt("dve", knew[kc_], t1[:, 0:T], t2[:, 0:T], ALU.add)
                        else:
                            P.tt("dve", kf[l][kc_][:, 128:128 + T], t1[:, 0:T], t2[:, 0:T], ALU.add)
                            if last:
                                P.tt("dve", kf32[kc_], t1[:, T - 128:T], t2[:, T - 128:T], ALU.add)
                                P.dma("sp", o_pk[l, kc_], kf32[kc_].ap, reads=[kf32[kc_]], dsem=3)
            wv = wget("w_in", l, 0, 8, [(C_VV, 256)])
            P.ser = True
            if not sample:
                for qb in range(T // 128):
                    pvv = bank()
                    for kc in range(8):
                        P.mm(pvv[:, 0:256], hb[kc][:, qb * 128:(qb + 1) * 128], wv[:, kc, 0:256], kc == 0, kc == 7)
                    P.copy("act", vtm[l][:, qb + 1, :], pvv[:, 0:256])
                    if last and qb == T // 128 - 1:
                        P.copy("dve", v32, pvv[:, 0:256])
                        P.dma("sp", o_pv[l], v32.ap, reads=[v32], dsem=3)
                for qb in range(T // 128):
                    first_blk = (ti == 0 and qb == 0)
                    kbs = [1] if first_blk else [0, 1]
                    for g in range(4):
                        kc_ = g // 2; hf = g % 2
                        rows = slice(hf * 64, hf * 64 + 64)
                        cbase = (g // 2) * 4
                        Es = []
                        for kb in kbs:
                            pS = bank()
                            for r in range(4):
                                P.mm(pS[:, r * 128:(r + 1) * 128],
                                     kf[l][kc_][rows, (qb + kb) * 128:(qb + kb + 1) * 128],
                                     qr[cbase + r][rows, qb * 128:(qb + 1) * 128], True, True, inc=(r == 3))
                            P.stt("dve", smb, pS, 0.125, maskOwn if kb == 1 else maskPrev, ALU.mult, ALU.add)
                            E = Ebf[kb]
                            P.act(E, smb, AF.Exp)
                            Es.append((kb, E))
                        po = bank(); pl = bank()
                        for i, (kb, E) in enumerate(Es):
                            P.mm(po[rows, :], vtm[l][:, qb + kb, g * 64:(g + 1) * 64], E, i == 0, i == len(Es) - 1)
                        for i, (kb, E) in enumerate(Es):
                            P.mm(pl[rows, :], onesB[:, 0:64], E, i == 0, i == len(Es) - 1)
                        den = tmpF[0]
                        P.tt("dve", V(den.ap[rows, :].rearrange("p (r q) -> p r q", r=4), den.keys),
                             V(pl.ap[rows, :].rearrange("p (r q) -> p r q", r=4), pl.keys),
                             V(esink[l].ap[rows, 4 * g:4 * g + 4].unsqueeze(2).to_broadcast([64, 4, 128]), esink[l].keys),
                             ALU.add)
                        P.recip(den[rows, :], den[rows, :])
                        for r in range(4):
                            P.tt("dve", a8[cbase + r][rows, qb * 128:(qb + 1) * 128], po[rows, r * 128:(r + 1) * 128],
                                 den[rows, r * 128:(r + 1) * 128], ALU.mult)
                for kc_ in range(2):
                    P.copy("dve", kf[l][kc_][:, 0:128], kf[l][kc_][:, T:T + 128])
                P.copy("dve", vtm[l][:, 0, :], vtm[l][:, T // 128, :])
            else:
                swa_sample(l, wv)
            chk(7)
            P.ser = not FREE_OK
            gated_out("wao", C_GCC, a8, 8, False)
            chk(8)

            def plain_out(wname, in_chunks, nk_in, rows_per_blk=8):
                nkb = (nk_in + 7) // 8
                for half in range(2):
                    pys = [bank() for _ in range(4)]
                    for kb in range(nkb):
                        nk = min(8, nk_in - kb * 8)
                        wb_ = wget(wname, l, kb * 1024, nk, [(half * 512, 512)])
                        for oc in range(4):
                            proj_fm(pys[oc][:, 0:T], wb_, nk, oc * 128, 128, in_chunks, T,
                                    start=(kb == 0), stop=(kb == nkb - 1), kc0=kb * 8)
                    for oc in range(4):
                        P.copy("act", macc[half * 4 + oc][:, 0:T], pys[oc][:, 0:T])

            for c in range(8):
                P.copy("act", a8[c][:, 0:T], macc[c][:, 0:T])
            plain_out("w_out", a8, 8)
            chk(8.5)
            post_norm_residual(l, "g_mix_post", T)
            chk(9)

            P.ser = True
            if (not sample) and ti == 0:
                mem_kv(l)
            P.ser = not FREE_OK
            norm_to_bf(xT, l, "g_x_pre", hb, T)
            for half in range(2):
                wb = wget("w_xq", l, 0, 8, [(half * 512, 512)])
                for oc in range(4):
                    pq = bank()
                    proj_fm(pq[:, 0:T], wb, 8, oc * 128, 128, hb, T)
                    P.copy("act", qr[half * 4 + oc][:, 0:T], pq[:, 0:T])
            P.ser = True
            if not sample:
                cross_core(memK[l], memV[l], 0, T, 0)
            else:
                cross_sample(l)
            P.ser = not FREE_OK
            plain_out("w_xo", a8, 8)
            post_norm_residual(l, "g_x_post", T)
            chk(10)

            norm_to_bf(xT, l, "g_ffn_pre", hb, T)
            for blk in range(6):
                ncol = 512 if blk < 5 else 256
                nch = ncol // 128
                wa = wget("w_ffn_in", l, 0, 8, [(blk * 512, ncol)])
                pas = [bank() for _ in range(nch)]
                for oc in range(nch):
                    proj_fm(pas[oc][:, 0:T], wa, 8, oc * 128, 128, hb, T)
                wg = wget("w_ffn_in", l, 0, 8, [(DFF + blk * 512, ncol)])
                for oc in range(nch):
                    c = blk * 4 + oc
                    pa = pas[oc]; pg = bank()
                    proj_fm(pg[:, 0:T], wg, 8, oc * 128, 128, hb, T)

                    def src(dst, pa=pa):
                        P.copy("act", dst, pa[:, 0:T])

                    def outf(rv, c=c, pg=pg):
                        g = gsb[c % 2]
                        P.act(g[:, 0:T], rv, AF.Silu, bias=pc(l, "ffn_conv_b", c))
                        P.tt("dve", ub[c][:, 0:T], pg[:, 0:T], g[:, 0:T], ALU.mult)
                    conv(src, "ffn_conv_w", None, 3, c, haloF, st_ff, o_sffn, outf)
            if last:
                P.dma("sp", o_pffn[l], haloF[l].ap, reads=[haloF[l]], dsem=3)
            plain_out("w_ffn_out", ub, NKF)
            post_norm_residual(l, "g_ffn_post", T)

        def cross_core(mK, mV, qcol0, N, ocol0):
            for hx in range(4):
                Es = []
                for kb in range(2):
                    pS = bank()
                    for dc in range(2):
                        P.mm(pS[:, 0:N], mK[2 * hx + dc][:, kb * 128:(kb + 1) * 128],
                             qr[2 * hx + dc][:, qcol0:qcol0 + N], dc == 0, dc == 1)
                    E = Ebf[kb]
                    P.act(E[:, 0:N], pS[:, 0:N], AF.Exp, scale=1.0 / 16.0)
                    Es.append(E)
                pl = bank()
                for kb in range(2):
                    P.mm(pl[:, 0:N], onesB, Es[kb][:, 0:N], kb == 0, kb == 1)
                den = tmpF[0]
                P.recip(den[:, 0:N], pl[:, 0:N])
                for dc in range(2):
                    po = bank()
                    for kb in range(2):
                        P.mm(po[:, 0:N], mV[:, kb, hx * 256 + dc * 128:hx * 256 + (dc + 1) * 128], Es[kb][:, 0:N],
                             kb == 0, kb == 1)
                    P.tt("dve", a8[2 * hx + dc][:, ocol0:ocol0 + N], po[:, 0:N], den[:, 0:N], ALU.mult)

        def mem_kv(l):
            mt = [ub[c] for c in range(8)]
            for c in range(8):
                P.dma("sp", macc[c].ap[:, 0:256], memT[c * 128:(c + 1) * 128, :], writes=[macc[c]], dsem=2)
            mn = [ub[8 + c] for c in range(8)]
            norm_to_bf(macc, l, "g_mem", mn, 256)
            for half in range(2):
                wb = wget("w_xk", l, 0, 8, [(half * 512, 512)])
                for oc in range(4):
                    c = half * 4 + oc
                    pk = bank()
                    proj_fm(pk[:, 0:256], wb, 8, oc * 128, 128, mn, 256)
                    P.copy("act", memK[l][c], pk[:, 0:256])
                    t = tmpF[c % 3]
                    P.copy("dve", t[:, 0:256], pk[:, 0:256])
                    P.dma("sp", o_pmk[l, c], t.ap[:, 0:256], reads=[t], dsem=3)
            for half in range(2):
                wb = wget("w_xv", l, 0, 8, [(half * 512, 512)])
                for kb in range(2):
                    pv_ = bank()
                    for kc in range(8):
                        P.mm(pv_, mn[kc][:, kb * 128:(kb + 1) * 128], wb[:, kc, 0:512], kc == 0, kc == 7)
                    P.copy("act", memV[l][:, kb, half * 512:(half + 1) * 512], pv_)
                    t = tmpF[(kb + 2 * half) % 3]
                    P.copy("dve", t, pv_)
                    P.dma("sp", o_pmv[l, kb * 128:(kb + 1) * 128, half * 512:(half + 1) * 512], t.ap, reads=[t], dsem=3)

        knew = [P.alloc([NS_], F32) for _ in range(2)]

        dtf = P.alloc([NS_], F32); decf = P.alloc([NS_], F32)
        dec_exp = P.alloc([16, NS_], F32); xdt_exp = P.alloc([16, NS_], F32); ysall = P.alloc([16, NS_], F32)
        ysm = P.alloc([8], F32)
        knew_tm = P.alloc([256], F32); vnew_tm = P.alloc([256], F32)
        ktm32 = P.alloc([256], F32); vtm32 = P.alloc([256], F32)
        kb16 = P.alloc([256], BF16); vb16 = P.alloc([256], BF16); kbT = P.alloc([2, 128], BF16)
        Esm = P.alloc([16], BF16); osm = P.alloc([NS_, 16], F32); dsm = P.alloc([NS_, 16], F32)

        def ssd_sample(l, wdt):
            pd = bank()
            for kc in range(8):
                P.mm(pd[0:32, 0:NS_], wdt[:, kc, 0:32], hb[kc][:, 0:NS_], kc == 0, kc == 7)
            P.act(dtf[0:32, :], pd[0:32, 0:NS_], AF.Exp, bias=pc(l, "dtb", 0)[0:32, :])
            P.act(dtf[0:32, :], dtf[0:32, :], AF.Ln, bias=1.0)
            P.ts("dve", decf[0:32, :], dtf[0:32, :], AcolS[l][0:32, :], ALU.mult)
            P.act(decf[0:32, :], decf[0:32, :], AF.Exp)
            pe_ = bank()
            for j in range(16):
                for hl in range(2):
                    lh = V(identF.ap[0:32, 2 * j + hl:2 * j + hl + 1].to_broadcast([32, 64]), identF.keys)
                    rws = slice(hl * 64, hl * 64 + 64)
                    P.mm(pe_[rws, j * 16:(j + 1) * 16], lh, decf[0:32, :], True, True, inc=False)
                    P.mm(pe_[rws, 256 + j * 16:256 + (j + 1) * 16], lh, dtf[0:32, :], True, True, inc=(j == 15 and hl == 1))
            P.copy("dve", dec_exp, V(pe_.ap[:, 0:256].rearrange("p (j b) -> p j b", j=16), pe_.keys))
            P.tt("dve", xdt_exp, xsf_all[:, :, 0:NS_], V(pe_.ap[:, 256:512].rearrange("p (j b) -> p j b", j=16), pe_.keys), ALU.mult)
            S = V(hT[0].ap.rearrange("p (j n) -> p j n", j=16), hT[0].keys)
            tmp = V(hTb.ap.bitcast(F32).rearrange("p (j n) -> p j n", j=8), hTb.keys)
            for b in range(NS_):
                P.dma("sp", S.ap, st_ssm[l, b].rearrange("(j q) n -> q j n", q=128), writes=[S], dsem=4)
                for half in range(2):
                    pB = bank(); pC = bank()
                    for gi in range(4):
                        g = half * 4 + gi
                        P.mm(pB[:, gi * 128:(gi + 1) * 128], V(Bf[g].ap[:, b:b + 1].to_broadcast([128, 128]), Bf[g].keys), identB,
                             True, True, inc=(gi == 3))
                    for gi in range(4):
                        g = half * 4 + gi
                        P.mm(pC[:, gi * 128:(gi + 1) * 128], V(Cf[g].ap[:, b:b + 1].to_broadcast([128, 128]), Cf[g].keys), identB,
                             True, True, inc=(gi == 3))
                    jh = slice(half * 8, half * 8 + 8)
                    Sh = V(S.ap[:, jh, :].rearrange("p (g r) n -> p g r n", r=2), S.keys)
                    tmp4 = V(tmp.ap.rearrange("p (g r) n -> p g r n", r=2), tmp.keys)

                    def bcn(v):
                        return V(v.ap.rearrange("p (g r) -> p g r", r=2).unsqueeze(3).to_broadcast([128, 4, 2, 128]), v.keys)

                    def bcg(pv):
                        return V(pv.ap.rearrange("p (g n) -> p g n", g=4).unsqueeze(2).to_broadcast([128, 4, 2, 128]), pv.keys)
                    P.tt("dve", Sh, Sh, bcn(dec_exp[:, jh, b]), ALU.mult)
                    P.tt("dve", tmp4, bcg(pB), bcn(xdt_exp[:, jh, b]), ALU.mult)
                    P.tt("dve", Sh, Sh, tmp4, ALU.add)
                    P.tt("dve", tmp4, Sh, bcg(pC), ALU.mult)
                    P.reduce(ysall[:, jh, b], tmp, ALU.add)
                P.dma("sp", o_sssm[l, b].rearrange("(j q) n -> q j n", q=128), S.ap, reads=[S], dsem=5)
            dcol = V(pcol[l].ap[:, PC["ssm_d"]:PC["ssm_d"] + 16].unsqueeze(2).to_broadcast([128, 16, NS_]), pcol[l].keys)
            P.tt("dve", xdt_exp, xsf_all[:, :, 0:NS_], dcol, ALU.mult)
            P.tt("dve", ub_all[:, :, 0:NS_], xdt_exp, ysall, ALU.add)

        def swa_sample(l, wv):
            pvv = bank()
            for kc in range(8):
                P.mm(pvv[0:NS_, 0:256], hb[kc][:, 0:NS_], wv[:, kc, 0:256], kc == 0, kc == 7)
            P.copy("dve", vnew_tm[0:NS_, :], pvv[0:NS_, 0:256])
            pk = bank()
            for kc_ in range(2):
                P.transpose(pk[0:NS_, kc_ * 128:(kc_ + 1) * 128], knew[kc_], identF, inc=(kc_ == 1))
            P.copy("dve", knew_tm[0:NS_, :], pk[0:NS_, 0:256])
            po = ps[6][:, 0:256]; pl = ps[6][:, 256:512]
            for b in range(NS_):
                P.dma("sp", ktm32.ap[0:127, :], c_sk[l, b, 1:128, :], writes=[ktm32], dsem=4)
                P.dma("sp", ktm32.ap[127:128, :], knew_tm.ap[b:b + 1, :], reads=[knew_tm], writes=[ktm32], dsem=4)
                P.dma("sp", vtm32.ap[0:127, :], c_sv[l, b, 1:128, :], writes=[vtm32], dsem=4)
                P.dma("sp", vtm32.ap[127:128, :], vnew_tm.ap[b:b + 1, :], reads=[vnew_tm], writes=[vtm32], dsem=4)
                P.dma("sp", o_sk[l, b], ktm32.ap, reads=[ktm32], dsem=5)
                P.dma("sp", o_sv[l, b], vtm32.ap, reads=[vtm32], dsem=5)
                P.copy("dve", kb16, ktm32)
                P.copy("dve", vb16, vtm32)
                for kc_ in range(2):
                    P.transpose(psb[:, kc_ * 128:(kc_ + 1) * 128], kb16[:, kc_ * 128:(kc_ + 1) * 128], identB, inc=(kc_ == 1))
                P.copy("act", kbT, V(psb.ap[:, 0:256].rearrange("p (c k) -> p c k", c=2), psb.keys))
                pS = bank()
                for g in range(4):
                    rows = slice((g % 2) * 64, (g % 2) * 64 + 64)
                    cbase = (g // 2) * 4
                    for r in range(4):
                        P.mm(pS[:, g * 4 + r:g * 4 + r + 1], kbT[rows, g // 2, :], qr[cbase + r][rows, b:b + 1],
                             True, True, inc=(g == 3 and r == 3))
                P.act(Esm, pS[:, 0:16], AF.Exp, scale=0.125)
                for g in range(4):
                    rows = slice((g % 2) * 64, (g % 2) * 64 + 64)
                    P.mm(po[rows, b * 16 + g * 4:b * 16 + g * 4 + 4], vb16[:, g * 64:(g + 1) * 64], Esm[:, g * 4:g * 4 + 4],
                         True, True, inc=False)
                    P.mm(pl[rows, b * 16 + g * 4:b * 16 + g * 4 + 4], onesB[:, 0:64], Esm[:, g * 4:g * 4 + 4],
                         True, True, inc=(g == 3))
            es = V(esink[l].ap.unsqueeze(1).to_broadcast([128, NS_, 16]), esink[l].keys)
            P.tt("dve", dsm, V(pl.ap.rearrange("p (b h) -> p b h", b=NS_), pl.keys), es, ALU.add)
            P.recip(dsm, dsm)
            P.tt("dve", osm, V(po.ap.rearrange("p (b h) -> p b h", b=NS_), po.keys), dsm, ALU.mult)
            for c in range(8):
                for half in range(2):
                    rows = slice(half * 64, half * 64 + 64)
                    h = myhead(c, half)
                    P.copy("dve", a8[c][rows, 0:NS_], osm[rows, :, h])

        def cross_sample(l):
            K32 = V(hT[1].ap.rearrange("p (k d) -> p k d", k=2), hT[1].keys)
            Kb = V(xdt_tm.ap.rearrange("p (k d) -> p k d", k=2), xdt_tm.keys)
            for b in range(NS_):
                P.dma("sp", K32.ap, c_mk[l, b].rearrange("(k q) d -> q k d", q=128), writes=[K32], dsem=4)
                P.copy("dve", Kb, K32)
                P.dma("sp", K32.ap, c_mv[l, b].rearrange("(k q) d -> q k d", q=128), writes=[K32], dsem=4)
                P.copy("dve", memV[l], K32)
                for kb in range(2):
                    for c in range(8):
                        P.transpose(psb[:, c * 128:(c + 1) * 128], Kb[:, kb, c * 128:(c + 1) * 128], identB, inc=(c == 7))
                    for c in range(8):
                        P.copy("act" if c % 2 == 0 else "dve", memK[l][c][:, kb * 128:(kb + 1) * 128], psb[:, c * 128:(c + 1) * 128])
                cross_core(memK[l], memV[l], b, 1, b)

        import os as _os
        tiles = [("p", ti) for ti in range(SEQ // TP)] + [("s", 0)]
        if debug_tiles is not None:
            tiles = debug_tiles
        layers = list(range(L)) if debug_layers is None else debug_layers
        for kind, ti in tiles:
            sample = kind == "s"
            T = NS_ if sample else TP
            for c in range(8):
                if sample:
                    P.dma("sp", xT[c].ap[:, 0:T], xs_in[c * 128:(c + 1) * 128, :], writes=[xT[c]], dsem=2)
                else:
                    P.dma("sp", xT[c].ap, xp[c * 128:(c + 1) * 128, ti * TP:(ti + 1) * TP], writes=[xT[c]], dsem=2)
            try:
                for l in layers:
                    layer(l, T, sample, ti)
            except _Stop:
                pass
            for c in range(8):
                if sample:
                    _src = {'macc': macc, 'hb': hb, 'a8': a8}.get(_os.environ.get('DUMP', ''), xT)[c]
                    if _src.ap.dtype != F32:
                        P.copy('dve', tmpF[0][:, 0:TP], _src)
                        _src = tmpF[0][:, 0:TP]
                    P.dma("sp", ys[c * 128:(c + 1) * 128, :], _src.ap[:, 0:T], reads=[_src], dsem=3)
                else:
                    _src = {'macc': macc, 'hb': hb, 'a8': a8}.get(_os.environ.get('DUMP', ''), xT)[c]
                    if _src.ap.dtype != F32:
                        P.copy('dve', tmpF[0][:, 0:TP], _src)
                        _src = tmpF[0][:, 0:TP]
                    P.dma("sp", yp[c * 128:(c + 1) * 128, ti * TP:(ti + 1) * TP], _src.ap, reads=[_src], dsem=3)
        P.emit(sems, block)
        import os as _os
        if _os.environ.get('DUMPTRACE'):
            for t in P.trace[-int(_os.environ['DUMPTRACE']):]:
                print('TR', t)
        print("ops per engine", P.n_ops, "final counts", P.final_counts, "arena", P.arena_off)
    return nc


def _consts():
    import ml_dtypes
    cst = np.zeros((128, NCS), np.float32)
    cst[:, 0:128] = np.eye(128, dtype=np.float32)
    s = np.arange(128)[:, None]; l_ = np.arange(128)[None, :]
    cst[:, 128:256] = (s <= l_).astype(np.float32)
    cst[:, 256:384] = 1.0
    own = np.where(s <= l_, 0.0, NEG).astype(np.float32)
    prev = np.where(s > l_, 0.0, NEG).astype(np.float32)
    cst[:, 384:896] = np.tile(own, (1, 4))
    cst[:, 896:1408] = np.tile(prev, (1, 4))
    cb = np.zeros((128, NCB), np.float32)
    cb[:, 0:128] = np.eye(128)
    cb[:, 128:256] = 1.0 / 1024
    cb[:, 256:384] = 1.0 / 2048
    cb[:, 384:512] = 1.0
    cb[:, 512:1024] = np.tile(np.where(l_ >= s, 0.0, NEG), (1, 4))
    cb[:, 1024:1152] = (s <= l_).astype(np.float32)
    cstb = cb.astype(ml_dtypes.bfloat16)

    def rope(pos):
        inv = (500000.0 ** (-np.arange(8, dtype=np.float32) / 8)).astype(np.float32)
        ang = pos.astype(np.float32)[:, None] * inv[None, :]
        cos = np.cos(ang).astype(np.float32); sin = np.sin(ang).astype(np.float32)
        out = np.zeros((128, 2, len(pos)), np.float32)
        for p in range(128):
            d = p % 64
            if d < 8:
                out[p, 0] = cos[:, d]; out[p, 1] = -sin[:, d]
            elif d < 16:
                out[p, 0] = cos[:, d - 8]; out[p, 1] = sin[:, d - 8]
            else:
                out[p, 0] = 1.0
        return out
    ropeP = rope(np.arange(SEQ)); ropeS = rope(np.full((NS_,), 8192))
    return cst, cstb, ropeP, ropeS


def _colmajor(v, n):
    return np.ascontiguousarray(v.reshape(n, 128).T)


def kernel(**inp):
    return _run(inp)


_PREP = {}


def _run(inp, debug_tiles=None, debug_layers=None, cores=None, stop=None):
    f32 = np.float32
    cst, cstb, ropeP, ropeS = _consts()
    w_in = inp["w_in"]
    wqk = np.zeros((L, D, 2560), f32)
    wao = np.zeros((L, D, D), f32)
    for c in range(8):
        for half in range(2):
            h = myhead(c, half)
            cols = C_Q + h * 64 + np.arange(64)
            part = cols.copy()
            part[0:8] = cols[8:16]; part[8:16] = cols[0:8]
            wqk[:, :, c * 256 + half * 64: c * 256 + half * 64 + 64] = w_in[:, :, cols]
            wqk[:, :, c * 256 + 128 + half * 64: c * 256 + 128 + half * 64 + 64] = w_in[:, :, part]
            wao[:, c * 128 + half * 64: c * 128 + half * 64 + 64, :] = inp["w_attn_out"][:, h * 64:(h + 1) * 64, :]
    for kc in range(2):
        for half in range(2):
            g = 2 * kc + half
            cols = C_K + g * 64 + np.arange(64)
            part = cols.copy()
            part[0:8] = cols[8:16]; part[8:16] = cols[0:8]
            o = 2048 + kc * 256
            wqk[:, :, o + half * 64:o + half * 64 + 64] = w_in[:, :, cols]
            wqk[:, :, o + 128 + half * 64:o + 128 + half * 64 + 64] = w_in[:, :, part]
    pcol = np.zeros((L, 128, NPC), f32)
    for l in range(L):
        def put(name, arr):
            pcol[l, :, PC[name]:PC[name] + arr.shape[1]] = arr
        put("g_mix_pre", _colmajor(inp["norm_mix_pre"][l], 8)); put("g_mix_post", _colmajor(inp["norm_mix_post"][l], 8))
        put("g_x_pre", _colmajor(inp["norm_x_pre"][l], 8)); put("g_x_post", _colmajor(inp["norm_x_post"][l], 8))
        put("g_ffn_pre", _colmajor(inp["norm_ffn_pre"][l], 8)); put("g_ffn_post", _colmajor(inp["norm_ffn_post"][l], 8))
        put("g_mem", _colmajor(inp["norm_mem"][l], 8))
        put("conv_a_w", np.concatenate([_colmajor(inp["conv_a_w"][l, j], 8) for j in range(3)], 1))
        put("ssm_conv_w", np.concatenate([_colmajor(inp["ssm_conv_w"][l, j], 32) for j in range(4)], 1))
        put("ssm_conv_b", _colmajor(inp["ssm_conv_b"][l], 32))
        put("ssm_norm", _colmajor(inp["ssm_norm"][l], 16))
        put("ssm_d", _colmajor(np.repeat(inp["ssm_d"][l], 64), 16))
        put("ffn_conv_w", np.concatenate([_colmajor(inp["ffn_conv_w"][l, j], NKF) for j in range(3)], 1))
        put("ffn_conv_b", _colmajor(inp["ffn_conv_b"][l], NKF))
        col = np.zeros((128, 1), f32); col[0:32, 0] = inp["ssm_dt_bias"][l]; put("dtb", col)
        col = np.zeros((128, 1), f32); col[0:32, 0] = inp["ssm_a_log"][l]; put("alog", col)
        put("sinks_bc", np.broadcast_to(inp["attn_sinks"][l][None, :], (128, 16)))
        put("alog_bc", np.broadcast_to(inp["ssm_a_log"][l][None, :], (128, 32)))
        put("dtb_bc", np.broadcast_to(inp["ssm_dt_bias"][l][None, :], (128, 32)))
    shared = {"w_in": w_in, "wqk": wqk, "w_a_out": inp["w_a_out"], "w_ssm_out": inp["w_ssm_out"], "wao": wao,
              "w_out": inp["w_out"], "w_xq": inp["w_xq"], "w_xk": inp["w_xk"], "w_xv": inp["w_xv"], "w_xo": inp["w_xo"],
              "w_ffn_in": inp["w_ffn_in"], "w_ffn_out": inp["w_ffn_out"], "pcol": pcol, "cst": cst, "cstb": cstb,
              "ropeP": ropeP, "ropeS": ropeS}
    shared = {k: np.ascontiguousarray(v) for k, v in shared.items()}

    def st_lay(a, nch, W_):
        x = a.reshape(L, NS_, W_, nch, 128)
        return np.ascontiguousarray(x.transpose(0, 4, 3, 2, 1).reshape(L, 128, nch, W_ * NS_))
    in_maps = []
    for c in range(8):
        bs = slice(c * NS_, (c + 1) * NS_)
        m = dict(shared)
        m["xp"] = np.ascontiguousarray(inp["x_prompt"][c].T)
        m["xs"] = np.ascontiguousarray(inp["x_sample"][bs, 0, :].T)
        m["memT"] = np.ascontiguousarray(inp["mem_prompt"][c].T)
        m["st_ca"] = st_lay(inp["state_conv_a"][:, bs], 8, 2)
        m["st_sc"] = st_lay(inp["state_ssm_conv"][:, bs], 32, 3)
        m["st_ff"] = st_lay(inp["state_ffn_conv"][:, bs], NKF, 2)
        m["st_ssm"] = np.ascontiguousarray(inp["state_ssm"][:, bs].reshape(L, NS_, 2048, 128))
        m["c_sk"] = np.ascontiguousarray(inp["cache_swa_k"][:, bs].reshape(L, NS_, 128, 256))
        m["c_sv"] = np.ascontiguousarray(inp["cache_swa_v"][:, bs].reshape(L, NS_, 128, 256))
        m["c_mk"] = np.ascontiguousarray(inp["cache_mem_k"][:, bs].reshape(L, NS_, 256, 1024))
        m["c_mv"] = np.ascontiguousarray(inp["cache_mem_v"][:, bs].reshape(L, NS_, 256, 1024))
        in_maps.append(m)
    nc = build_program(debug_tiles, debug_layers, stop)
    if cores is not None:
        res = run_bass_kernel_spmd(nc, [in_maps[c] for c in cores], core_ids=list(range(len(cores))))
        R = [res.results[0]] * 8
        for i, c in enumerate(cores):
            R[c] = res.results[i]
        return assemble(R)
    res = run_bass_kernel_spmd(nc, in_maps, core_ids=list(range(8)))
    R = res.results
    return assemble(R)


def assemble(R):
    f32 = np.float32
    yp = np.stack([R[c]["yp"].T for c in range(8)])
    ys = np.concatenate([R[c]["ys"].T[:, None, :] for c in range(8)], 0)

    def un_p(key, nch, W_):
        return np.stack([R[c][key].transpose(0, 3, 2, 1).reshape(L, W_, nch * 128) for c in range(8)], 1)

    def un_s(key, nch, W_):
        outs = []
        for c in range(8):
            a = R[c][key].reshape(L, 128, nch, W_, NS_).transpose(0, 4, 3, 2, 1).reshape(L, NS_, W_, nch * 128)
            outs.append(a)
        return np.concatenate(outs, 1)
    p_conv_a = un_p("o_pca", 8, 2); p_ssm_conv = un_p("o_psc", 32, 3); p_ffn = un_p("o_pffn", NKF, 2)
    p_ssm = np.stack([R[c]["o_pssm"].reshape(L, 128, 32, 64).transpose(0, 2, 3, 1) for c in range(8)], 1)
    p_k = np.stack([R[c]["o_pk"].transpose(0, 3, 1, 2).reshape(L, 128, 4, 64) for c in range(8)], 1)
    p_v = np.stack([R[c]["o_pv"].reshape(L, 128, 4, 64) for c in range(8)], 1)
    p_mk = np.stack([R[c]["o_pmk"].transpose(0, 3, 1, 2).reshape(L, 256, 4, 256) for c in range(8)], 1)
    p_mv = np.stack([R[c]["o_pmv"].reshape(L, 256, 4, 256) for c in range(8)], 1)
    s_conv_a = un_s("o_sca", 8, 2); s_ssm_conv = un_s("o_ssc", 32, 3); s_ffn = un_s("o_sffn", NKF, 2)
    s_ssm = np.concatenate([R[c]["o_sssm"].reshape(L, NS_, 32, 64, 128) for c in range(8)], 1)
    s_k = np.concatenate([R[c]["o_sk"].reshape(L, NS_, 128, 4, 64) for c in range(8)], 1)
    s_v = np.concatenate([R[c]["o_sv"].reshape(L, NS_, 128, 4, 64) for c in range(8)], 1)
    outs = (yp, ys, p_conv_a, p_ssm_conv, p_ssm, p_k, p_v, p_ffn, p_mk, p_mv,
            s_conv_a, s_ssm_conv, s_ssm, s_k, s_v, s_ffn)
    return tuple(np.ascontiguousarray(o, dtype=f32) for o in outs)
```
